# Optimizing a Trainium2 kernel written in Bass

```python
import math
import jax, jax.numpy as jnp
from jax import lax
import numpy as np

D_MODEL = 2048
BATCH = 4
SEQ = 2048
DEPTH = 1
DEC_BATCH = 128
DEC_SEQ = 8
PAST_LEN = 16384
PAGE_SIZE = 128

POOL_WINDOWS = (2, 4, 8, 16)
POOL_MAX = 16
D_POOL = D_MODEL // 2
POOL_GROUP = D_POOL // 4
POOL_OUT_GROUP = D_MODEL // 4
N_HEADS = 8
DK = 128
DV = 256
D_QK = N_HEADS * DK
D_V = N_HEADS * DV
RET_CHUNK = 128
ROPE_BASE = 10000.0
D_FF = 5632
CONV_K = 3
EPS = 1e-6

IN_SPLITS = (D_POOL, D_QK, D_QK, D_V, D_V, D_MODEL, D_MODEL)
N_IN = sum(IN_SPLITS)

kernel_name = "hybrid_pool_retention_convffn_step"


def _split_points(sizes):
    pts, acc = [], 0
    for s in sizes[:-1]:
        acc += s
        pts.append(acc)
    return pts


def rmsnorm(x, g):
    x32 = x.astype(jnp.float32)
    y = x32 * lax.rsqrt(jnp.mean(x32 * x32, axis=-1, keepdims=True) + EPS)
    return (y * g.astype(jnp.float32)).astype(x.dtype)


def rotary(x, pos):
    half = DK // 2
    theta = ROPE_BASE ** (-jnp.arange(half, dtype=jnp.float32) / half)
    ang = pos.astype(jnp.float32)[:, None] * theta[None, :]
    cos = jnp.cos(ang)[None, :, None, :]
    sin = jnp.sin(ang)[None, :, None, :]
    x32 = x.astype(jnp.float32)
    x1, x2 = x32[..., :half], x32[..., half:]
    return jnp.concatenate([x1 * cos - x2 * sin, x2 * cos + x1 * sin], axis=-1)


def pool_mix(u_ext, pos, w_pool, pool_scale):
    P = POOL_MAX
    L = u_ext.shape[1] - (P - 1)
    u32 = u_ext.astype(jnp.float32)
    cz = jnp.concatenate([jnp.zeros_like(u32[:, :1]), jnp.cumsum(u32, axis=1)], axis=1)
    u = u32[:, P - 1:]
    outs = []
    for g, w in enumerate(POOL_WINDOWS):
        sl = slice(g * POOL_GROUP, (g + 1) * POOL_GROUP)
        wsum = cz[:, P:P + L, sl] - cz[:, P - w:P - w + L, sl]
        cnt = jnp.minimum(w, pos + 1).astype(jnp.float32)[None, :, None]
        z = wsum / cnt - u[:, :, sl]
        outs.append(jnp.einsum('bld,de->ble', z, w_pool[g].astype(jnp.float32)))
    a = jnp.concatenate(outs, axis=-1) * pool_scale.astype(jnp.float32)
    return a.astype(u_ext.dtype)


def retention(q, k, v, state0):
    B, L = q.shape[0], q.shape[1]
    C = RET_CHUNK if L % RET_CHUNK == 0 else L
    n = L // C
    log_g = jnp.log(1.0 - 2.0 ** (-5.0 - jnp.arange(N_HEADS, dtype=jnp.float32)))
    idx = jnp.arange(C, dtype=jnp.float32)
    rel = idx[:, None] - idx[None, :]
    dmask = jnp.where(rel >= 0, jnp.exp(jnp.maximum(rel, 0.0)[None] * log_g[:, None, None]), 0.0)
    xi = jnp.exp((idx + 1.0)[None, :] * log_g[:, None])
    zeta = jnp.exp((C - 1.0 - idx)[None, :] * log_g[:, None])
    g_c = jnp.exp(C * log_g)

    def to_chunks(t):
        d = t.shape[-1]
        return t.reshape(B, n, C, N_HEADS, d).transpose(1, 0, 3, 2, 4)

    def step(R, inp):
        qc, kc, vc = inp
        s = jnp.einsum('bhid,bhjd->bhij', qc, kc) * dmask[None]
        o = jnp.einsum('bhij,bhjv->bhiv', s, vc) + \
            jnp.einsum('bhid,bhdv->bhiv', qc, R) * xi[None, :, :, None]
        R = R * g_c[None, :, None, None] + \
            jnp.einsum('bhjd,bhjv->bhdv', kc * zeta[None, :, :, None], vc)
        return R, o

    R, o = lax.scan(step, state0.astype(jnp.float32), (to_chunks(q), to_chunks(k), to_chunks(v)))
    o = o.transpose(1, 0, 3, 2, 4).reshape(B, L, N_HEADS, DV)
    return o, R


def causal_dwconv(u_ext, conv_w, conv_b):
    L = u_ext.shape[1] - (CONV_K - 1)
    y = conv_b[None, None, :]
    for j in range(CONV_K):
        y = y + u_ext[:, j:j + L] * conv_w[j][None, None, :]
    return y


def layer(x, pos, pool_buf, ret_state, conv_buf,
          g_pre_mix, w_in, w_pool, pool_scale, gn_gain, w_out, g_post_mix,
          g_pre_ffn, w_up, conv_w, conv_b, w_down, g_post_ffn):
    B, L, _ = x.shape
    h = rmsnorm(x, g_pre_mix)
    proj = jnp.einsum('bld,dn->bln', h, w_in)
    u_pool, q, k, v, g_ret, g_a, g_r = jnp.split(proj, _split_points(IN_SPLITS), axis=-1)

    pool_ext = jnp.concatenate([pool_buf.astype(u_pool.dtype), u_pool], axis=1)
    a = pool_mix(pool_ext, pos, w_pool, pool_scale)

    q = rotary(q.reshape(B, L, N_HEADS, DK), pos)
    k = rotary(k.reshape(B, L, N_HEADS, DK), pos) * (DK ** -0.5)
    v = v.reshape(B, L, N_HEADS, DV).astype(jnp.float32)
    o, R = retention(q, k, v, ret_state)
    mu = jnp.mean(o, axis=-1, keepdims=True)
    var = jnp.mean(jnp.square(o - mu), axis=-1, keepdims=True)
    o = ((o - mu) * lax.rsqrt(var + EPS)).reshape(B, L, D_V) * gn_gain.astype(jnp.float32)
    r = (jax.nn.silu(g_ret.astype(jnp.float32)) * o).astype(x.dtype)

    m = jax.nn.sigmoid(g_a) * a + jax.nn.sigmoid(g_r) * r
    x1 = x + rmsnorm(jnp.einsum('bld,de->ble', m, w_out), g_post_mix)

    h2 = rmsnorm(x1, g_pre_ffn)
    up = jnp.einsum('bld,df->blf', h2, w_up)
    up_ext = jnp.concatenate([conv_buf.astype(up.dtype), up], axis=1)
    c = causal_dwconv(up_ext, conv_w, conv_b)
    val, gate = c[..., :D_FF], c[..., D_FF:]
    f = jnp.einsum('blf,fd->bld', jax.nn.gelu(gate, approximate=True) * val, w_down)
    y = x1 + rmsnorm(f, g_post_ffn)

    new_pool = pool_ext[:, -(POOL_MAX - 1):]
    new_conv = up_ext[:, -(CONV_K - 1):]
    return y, new_pool, R.astype(x.dtype), new_conv


def setup_inputs(seed: int = 0) -> dict:
    key = jax.random.key(seed)
    ks = jax.random.split(key, 20)
    f32 = jnp.float32
    nrm = lambda k, s: jax.random.normal(k, s, f32)
    return {
        "x_prompt": nrm(ks[0], (BATCH, SEQ, D_MODEL)),
        "x_sample": nrm(ks[1], (DEC_BATCH, DEC_SEQ, D_MODEL)),
        "state_pool": nrm(ks[2], (DEC_BATCH, POOL_MAX - 1, D_POOL)),
        "state_ret": 0.1 * nrm(ks[3], (DEC_BATCH, N_HEADS, DK, DV)),
        "state_conv": nrm(ks[4], (DEC_BATCH, CONV_K - 1, 2 * D_FF)),
        "g_pre_mix": 1.0 + 0.02 * nrm(ks[5], (D_MODEL,)),
        "w_in": nrm(ks[6], (D_MODEL, N_IN)) * D_MODEL ** -0.5,
        "w_pool": nrm(ks[7], (4, POOL_GROUP, POOL_OUT_GROUP)) * POOL_GROUP ** -0.5,
        "pool_scale": 1.0 + 0.1 * nrm(ks[8], (D_MODEL,)),
        "gn_gain": 1.0 + 0.02 * nrm(ks[9], (D_V,)),
        "w_out": nrm(ks[10], (D_MODEL, D_MODEL)) * D_MODEL ** -0.5,
        "g_post_mix": 1.0 + 0.02 * nrm(ks[11], (D_MODEL,)),
        "g_pre_ffn": 1.0 + 0.02 * nrm(ks[12], (D_MODEL,)),
        "w_up": nrm(ks[13], (D_MODEL, 2 * D_FF)) * D_MODEL ** -0.5,
        "conv_w": nrm(ks[14], (CONV_K, 2 * D_FF)) * CONV_K ** -0.5,
        "conv_b": 0.01 * nrm(ks[15], (2 * D_FF,)),
        "w_down": nrm(ks[16], (D_FF, D_MODEL)) * D_FF ** -0.5,
        "g_post_ffn": 1.0 + 0.02 * nrm(ks[17], (D_MODEL,)),
    }


def reference(x_prompt, x_sample, state_pool, state_ret, state_conv,
              g_pre_mix, w_in, w_pool, pool_scale, gn_gain, w_out, g_post_mix,
              g_pre_ffn, w_up, conv_w, conv_b, w_down, g_post_ffn):
    weights = (g_pre_mix, w_in, w_pool, pool_scale, gn_gain, w_out, g_post_mix,
               g_pre_ffn, w_up, conv_w, conv_b, w_down, g_post_ffn)
    Bp, Lp = x_prompt.shape[0], x_prompt.shape[1]
    Ls = x_sample.shape[1]
    pos_p = jnp.arange(Lp, dtype=jnp.int32)
    pos_s = PAST_LEN + jnp.arange(Ls, dtype=jnp.int32)

    yp, sp_pool, sp_ret, sp_conv = x_prompt, None, None, None
    ys, ss_pool, ss_ret, ss_conv = x_sample, None, None, None
    for _ in range(DEPTH):
        yp, sp_pool, sp_ret, sp_conv = layer(
            yp, pos_p,
            jnp.zeros((Bp, POOL_MAX - 1, D_POOL), x_prompt.dtype),
            jnp.zeros((Bp, N_HEADS, DK, DV), x_prompt.dtype),
            jnp.zeros((Bp, CONV_K - 1, 2 * D_FF), x_prompt.dtype),
            *weights)
        ys, ss_pool, ss_ret, ss_conv = layer(
            ys, pos_s, state_pool, state_ret, state_conv, *weights)
    return (yp, ys, sp_pool, sp_ret, sp_conv, ss_pool, ss_ret, ss_conv)
```

```python
from contextlib import ExitStack

import numpy as np
import concourse.bass as bass
import concourse.mybir as mybir
from concourse.bass_utils import run_bass_kernel_spmd

F32 = mybir.dt.float32
BF16 = mybir.dt.bfloat16
AF = mybir.ActivationFunctionType
ALU = mybir.AluOpType

D = 2048
NIN = 11264
DFF = 5632
H = 8
DK = 128
DV = 256
EPS = 1e-6
NJ = 44
NCORES = 8
GAMMA = [1.0 - 2.0 ** (-5.0 - h) for h in range(H)]
GC_P = [g ** 128 for g in GAMMA]
GC_S = [g ** 8 for g in GAMMA]

CB_ORDER_FULL = [2, 3, 4, 5, 6, 7, 8, 9, 0, 1] + list(range(10, 22))
CB_ORDER_KV = [4, 5, 6, 7, 8, 9]


class Trk:
    __slots__ = ("name", "w", "r", "excl")

    def __init__(self, name, excl=False):
        self.name = name
        self.w = None
        self.r = {}
        self.excl = excl


class Slot:
    __slots__ = ("name", "trks", "sem", "count")

    def __init__(self, name, trks, sem):
        self.name = name
        self.trks = trks
        self.sem = sem
        self.count = 0


class Eng:
    def __init__(self, key, sem):
        self.key = key
        self.sem = sem
        self.count = 0
        self.ops = []
        self.seen = {}


class Builder:
    def __init__(self, nc, es):
        self.nc = nc
        self.es = es
        self.eng = {}
        for k in ("pe", "act", "dve", "pool", "sp"):
            self.eng[k] = Eng(k, es.enter_context(nc.semaphore("sem_" + k)))
        self.slots = []
        self.out_slots = []

    def trk(self, name, excl=False):
        return Trk(name, excl)

    def slot(self, name, trks):
        s = Slot(name, list(trks), self.es.enter_context(self.nc.semaphore("ds_" + name)))
        self.slots.append(s)
        return s

    def _deps(self, e, reads, writes):
        best = {}

        def add(ev):
            if ev is None:
                return
            sem, val, key = ev
            if key == "pe" and e.key == "pe":
                return
            k = id(sem)
            if k not in best or best[k][1] < val:
                best[k] = ev

        for t in reads:
            add(t.w)
        for t in writes:
            add(t.w)
            for ev in t.r.values():
                add(ev)
        waits = []
        for k, (sem, val, key) in best.items():
            if e.seen.get(k, 0) >= val:
                continue
            e.seen[k] = val
            waits.append((sem, val))
        return waits

    @staticmethod
    def _record(ev, reads, writes):
        k = id(ev[0])
        for t in reads:
            t.r[k] = ev
        for t in writes:
            t.w = ev
            t.r = {}

    def op(self, ek, fn, reads=(), writes=()):
        e = self.eng[ek]
        if any(t.excl for t in reads):
            writes = list(writes) + [t for t in reads if t.excl and t not in writes]
            reads = [t for t in reads if not t.excl]
        waits = self._deps(e, reads, writes)
        e.count += 1
        ev = (e.sem, e.count, ek)
        self._record(ev, reads, writes)
        e.ops.append((waits, fn, e.sem, 1))

    def dma(self, qk, out, in_, slot, kind, extra_reads=(), is_output=False, extra_writes=()):
        q = self.eng[qk]
        if kind == "load":
            reads, writes = list(extra_reads), list(slot.trks) + list(extra_writes)
        else:
            reads, writes = list(slot.trks) + list(extra_reads), list(extra_writes)
        waits = self._deps(q, reads, writes)
        slot.count += 16
        ev = (slot.sem, slot.count, None)
        self._record(ev, reads, writes)
        q.ops.append((waits, (lambda e, o=out, i=in_: e.dma_start(out=o, in_=i)), slot.sem, 16))
        if is_output and slot not in self.out_slots:
            self.out_slots.append(slot)

    def wait_slot_all(self, slot):
        for e in self.eng.values():
            if e.key in ("sp", "pool"):
                continue
            k = id(slot.sem)
            if e.seen.get(k, 0) >= slot.count:
                continue
            e.seen[k] = slot.count
            e.ops.append(([(slot.sem, slot.count)], None, None, 0))

    def finalize(self):
        nc = self.nc
        sp = self.eng["sp"]
        fin = [(s.sem, s.count) for s in self.out_slots]
        handles = {"pe": "tensor", "act": "scalar", "dve": "vector", "pool": "gpsimd", "sp": "sync"}
        block = self.es.enter_context(nc.Block())

        def make(ek):
            eng = self.eng[ek]

            def body(e):
                for waits, fn, sem, n in eng.ops:
                    for (ws, wv) in waits:
                        e.wait_ge(ws, wv)
                    if fn is None:
                        continue
                    ins = fn(e)
                    ins.then_inc(sem, n)
                if ek == "sp":
                    for (ws, wv) in fin:
                        e.wait_ge(ws, wv)
            return body

        for ek, hn in handles.items():
            getattr(block, hn)(make(ek))


def bc(ap, pos, n):
    dims = [list(d) for d in ap.ap]
    dims.insert(1 + pos, [0, n])
    return bass.AP(ap.tensor, ap.offset, dims)


class _Stop(Exception):
    pass


def build_program(debug=None):
    nc = bass.Bass("TRN2", target_bir_lowering=False)

    def ck(name):
        if debug is not None and debug in (name, name + "_x"):
            raise _Stop()

    def din(name, shape):
        return nc.dram_tensor(name, list(shape), F32, kind="ExternalInput").ap()

    def dout(name, shape):
        return nc.dram_tensor(name, list(shape), F32, kind="ExternalOutput").ap()

    xall = din("xall", [2048, D])
    xs = din("xs", [128, D])
    sp_in = din("sp", [120, 2, 1024])
    spraw = din("spraw", [16, 15, 1024])
    sr = din("sr", [16, H, DK, DV])
    sc = din("sc", [32, NIN])
    w_in = din("w_in", [D, NIN])
    w_pool = din("w_pool", [4, 256, 512])
    w_out = din("w_out", [D, D])
    w_up = din("w_up", [D, NIN])
    w_down = din("w_down", [DFF, D])
    gvec = din("gvec", [4, D])
    gT_in = din("gT", [128, 32])
    convtab_in = din("convtab", [128, 88 * 4])
    cs_in = din("cs", [17, 128, 128])
    consts_in = din("consts", [128, 448])
    bands_in = din("bands", [128, 4 * 512])
    bandh_in = din("bandh", [120, 2 * 512])

    wsc = nc.dram_tensor("wsc", [64, 128, 8192], BF16).ap()

    y_main = dout("y_main", [1024, D])
    y_s = dout("y_s", [128, D])
    np_p = dout("np_p", [15, 1024])
    nr_p = dout("nr_p", [H, DK, DV])
    nc_p = dout("nc_p", [2, NIN])
    np_s = dout("np_s", [16, 15, 1024])
    nr_s = dout("nr_s", [16, H, DK, DV])
    nc_s = dout("nc_s", [32, NIN])

    es = ExitStack()
    with es:
        off = [0]

        def alloc(nbytes):
            o = off[0]
            off[0] += (nbytes + 31) // 32 * 32
            return o

        A = {}
        for t in range(4):
            A["X%d" % t] = alloc(8192)
            A["Y%d" % t] = alloc(8192)
        A["hT"] = alloc(16384)
        A["WB0"] = alloc(16384)
        A["WB1"] = alloc(16384)
        A["WB2"] = alloc(16384)
        A["R"] = alloc(8192)
        A["carryU"] = alloc(2048)
        A["identb"] = alloc(256)
        A["consts"] = alloc(448 * 4)
        A["gT"] = alloc(128)
        A["convtab"] = alloc(88 * 4 * 4)
        A["hist"] = alloc(88 * 2 * 4)
        A["cs"] = alloc(4 * 512)
        A["small"] = alloc(64 * 4)
        A["gnst"] = alloc(8 * 6 * 4 + 8 * 2 * 4 + 64 + 64)
        A["xb0"] = alloc(4096)
        A["xb1"] = alloc(4096)
        shared_base = off[0]
        A["GB"] = alloc(8192)
        A["rot"] = alloc(2048)
        A["t1"] = alloc(1024)
        A["t2"] = alloc(1024)
        A["sg0"] = alloc(2048)
        A["sg1"] = alloc(2048)
        A["zT"] = alloc(2048)
        A["qkT"] = alloc(4096)
        A["STm"] = alloc(2048)
        A["Rbf"] = alloc(4096)
        A["wpool"] = alloc(8192)
        A["bands"] = alloc(4096)
        A["bandh"] = alloc(2048)
        A["Rb0"] = alloc(2048)
        A["Rb1"] = alloc(2048)
        A["qTm0"] = alloc(2048)
        A["qTm1"] = alloc(2048)
        A["kppm0"] = alloc(2048)
        A["kpp"] = A["kppm0"]
        A["kppm1"] = alloc(2048)
        A["spbf"] = A["qkT"]
        endA = off[0]
        off[0] = shared_base
        A["actT"] = alloc(22 * 512 * 2)
        for nm in ("upv0", "upv1", "upg0", "upg1"):
            A[nm] = alloc(516 * 4)
        for nm in ("usv0", "usv1", "usg0", "usg1"):
            A[nm] = alloc(640)
        for nm in ("cv0", "cv1", "cg0", "cg1"):
            A[nm] = alloc(2048)
        A["gl0"], A["gl1"] = A["cg0"], A["cg1"]
        A["scT"] = alloc(88 * 32 * 4)
        A["stg0"], A["stg1"] = A["cv0"], A["cv1"]
        endB = off[0]
        total = max(endA, endB)
        assert total <= 212800, (total, endA, endB)
        arena = es.enter_context(nc.sbuf_tensor("arena", [128, total // 4], F32))
        ps = es.enter_context(nc.psum_tensor("ps", [128, 4096], F32))

        def V(name, nbytes, dt=F32, pat=None, boff=0, **kw):
            o = A[name] + boff
            ap = arena[:, o // 4:(o + nbytes) // 4]
            if dt == BF16:
                ap = ap.bitcast(BF16)
            if pat:
                ap = ap.rearrange(pat, **kw)
            return ap

        def PSf(b0, nb=1):
            return ps[:, b0 * 512:(b0 + nb) * 512]

        def PSb(b0, nb=1):
            return ps[:, b0 * 512:(b0 + nb) * 512].bitcast(BF16)

        K = Builder(nc, es)

        PST = [K.trk("ps%d" % b, excl=True) for b in range(8)]
        Xq = [K.trk("Xq%d" % t) for t in range(4)]
        Xk = [K.trk("Xk%d" % t) for t in range(4)]
        Xv = [K.trk("Xv%d" % t) for t in range(4)]
        XT = [[Xq[t], Xk[t], Xv[t]] for t in range(4)]
        YT = [K.trk("Y%d" % t) for t in range(4)]
        hTT = [K.trk("hT%d" % t) for t in range(4)]
        NHB = 6
        HBT = [K.trk("HB%d" % k) for k in range(NHB)]
        T = {n: K.trk(n) for n in (
            "GB", "R", "carryU", "const", "hist", "cs", "small", "gnst", "xb0", "xb1", "rot", "t1", "t2",
            "sg0", "sg1", "zT", "qkT", "STm", "kpp", "Rbf", "wpool", "bands", "Rf0", "Rf1", "Rb0", "Rb1",
            "qTm0", "qTm1", "kppm0", "kppm1", "spbf", "actT", "upv0", "upv1", "upg0", "upg1", "usv0", "usv1",
            "usg0", "usg1", "cv0", "cv1", "cg0", "cg1", "gl0", "gl1", "scT", "stg0", "stg1")}
        T["kpp"] = T["kppm0"]
        T["spbf"] = T["qkT"]
        T["gl0"], T["gl1"] = T["cg0"], T["cg1"]
        T["stg0"], T["stg1"] = T["cv0"], T["cv1"]
        PHA = ["GB", "rot", "t1", "t2", "sg0", "sg1", "zT", "qkT", "STm", "kpp", "Rbf", "wpool", "bands", "Rf0", "Rf1",
               "Rb0", "Rb1", "qTm0", "qTm1", "kppm0", "kppm1", "spbf"]
        PHB = ["actT", "upv0", "upv1", "upg0", "upg1", "usv0", "usv1", "usg0", "usg1", "cv0", "cv1", "cg0", "cg1",
               "gl0", "gl1", "scT", "stg0", "stg1"]

        actTT = [K.trk("actT%d" % j) for j in range(22)]
        for j in range(4):
            T["small%d" % j] = K.trk("small%d" % j)
        for j in range(22):
            T["actT%d" % j] = actTT[j]
        PHB += ["actT%d" % j for j in range(22)]
        S_X = [K.slot("X%d" % t, XT[t]) for t in range(4)]
        S_Y = [K.slot("Y%d" % t, [YT[t]]) for t in range(4)]
        S_HB = [K.slot("HB%d" % k, [HBT[k]]) for k in range(NHB)]
        S_HBst = [K.slot("HBst%d" % k, [HBT[k]]) for k in range(NHB)]
        S_GB = K.slot("GB", [T["GB"]])
        S_R = K.slot("R", [T["R"]])
        S_GBf = K.slot("GBf", [T["cv0"], T["cv1"], T["cg0"], T["cg1"]])
        S_const = K.slot("const", [T["const"]])
        S_cs = K.slot("cs", [T["cs"]])
        S_wpool = K.slot("wpool", [T["wpool"]])
        S_bands = K.slot("bands", [T["bands"]])
        S_spbf = K.slot("spbf", [T["spbf"]])
        S_Rb = [K.slot("Rb%d" % i, [T["Rb%d" % i]]) for i in range(2)]
        S_sg = [K.slot("sg%d" % i, [T["sg%d" % i]]) for i in range(2)]
        S_stg = [K.slot("stg%d" % i, [T["stg%d" % i]]) for i in range(2)]
        S_po = [K.slot("po%d" % i, [T["sg%d" % i]]) for i in range(2)]
        S_misc = K.slot("misc", [])
        YH = [K.trk("YH%d" % i) for i in range(8)]
        S_YH = [K.slot("YH%d" % i, [YH[i]]) for i in range(8)]
        S_YHst = [K.slot("YHst%d" % i, [YH[i]]) for i in range(8)]

        Xf = [V("X%d" % t, 8192) for t in range(4)]
        Xqb = [V("X%d" % t, 2048, BF16) for t in range(4)]
        Xkb = [V("X%d" % t, 2048, BF16, boff=2048) for t in range(4)]
        Xvb = [V("X%d" % t, 4096, BF16, boff=4096) for t in range(4)]
        Yf = [V("Y%d" % t, 8192) for t in range(4)]
        Yub = [V("Y%d" % t, 2048, BF16, boff=6144) for t in range(4)]
        YHv = [V("Y%d" % (i // 2), 4096, F32, "p (h v) -> p h v", h=4, boff=4096 * (i % 2)) for i in range(8)]
        hT = V("hT", 16384, BF16, "p (k n) -> p k n", k=16)
        HBv = [V("WB0", 8192, BF16, "p (k n) -> p k n", k=8, boff=8192 * k) for k in range(NHB)]
        HBf = [V("WB0", 8192, BF16, boff=8192 * k) for k in range(NHB)]

        def hb(i, h):
            return (2 * i + h) % NHB
        GB = V("GB", 8192)
        GBf = V("cv0", 8192)
        Rf32 = V("R", 8192, F32, "p (h v) -> p h v", h=8)
        carryU = V("carryU", 2048, BF16)
        identb = V("identb", 256, BF16)
        consts = V("consts", 448 * 4)
        identf = consts[:, 0:128]
        cmask = [consts[:, 128:256], consts[:, 256:384]]
        rowmask = consts[:, 384:400]
        dec = consts[:, 400:448].rearrange("p (a h) -> p a h", a=6)
        gT = V("gT", 128, F32, "p (a k) -> p a k", a=2)
        convtab = V("convtab", 88 * 16, F32, "p (j c) -> p j c", c=4)
        hist = V("hist", 88 * 8, F32, "p (j c) -> p j c", c=2)
        cst = V("cs", 2048, F32, "p (t c) -> p t c", t=4)
        small = V("small", 256)
        st6 = V("gnst", 192, F32, "p (h c) -> p h c", c=6)
        mv = V("gnst", 64, F32, "p (h c) -> p h c", c=2, boff=192)
        rs8 = V("gnst", 32, F32, boff=256)
        nm8 = V("gnst", 32, F32, boff=288)
        xb = [V("xb%d" % i, 4096, BF16) for i in range(2)]
        rot = V("rot", 2048, F32, "p (h d) -> p h d", h=4)
        t1 = V("t1", 1024, F32, "p (h d) -> p h d", h=4)
        t2 = V("t2", 1024, F32, "p (h d) -> p h d", h=4)
        sg = [V("sg%d" % i, 2048) for i in range(2)]
        zT = V("zT", 2048, BF16, "p (c n) -> p c n", c=8)
        qkT = V("qkT", 4096, BF16, "p (c n) -> p c n", c=16)
        STm = V("STm", 2048, BF16, "p (h n) -> p h n", h=8)
        kpp = V("kpp", 2048, BF16, "p (h n) -> p h n", h=8)
        Rbf = V("Rbf", 4096, BF16, "p (h v) -> p h v", h=8)
        wpool = V("wpool", 8192, BF16, "p (c n) -> p c n", c=8)
        bands = V("bands", 4096, BF16, "p (b w n) -> p b w n", b=4, w=4)
        bandh = V("bandh", 2048, BF16, "p (b w n) -> p b w n", b=2, w=4)
        RbS = [V("Rb%d" % i, 2048, BF16, "p (h v) -> p h v", h=4) for i in range(2)]
        qTm = [V("qTm%d" % i, 2048, BF16, "p (h n) -> p h n", h=8) for i in range(2)]
        kppm = [V("kppm%d" % i, 2048, BF16, "p (h n) -> p h n", h=8) for i in range(2)]
        spbf = V("spbf", 4096, BF16, "p (a n) -> p a n", a=2)
        actT = V("actT", 22 * 1024, BF16, "p (j n) -> p j n", j=22)
        upb = {("v", i): V("upv%d" % i, 516 * 4) for i in range(2)}
        upb.update({("g", i): V("upg%d" % i, 516 * 4) for i in range(2)})
        usb = {("v", i): V("usv%d" % i, 640, F32, "p (s c) -> p s c", s=16) for i in range(2)}
        usb.update({("g", i): V("usg%d" % i, 640, F32, "p (s c) -> p s c", s=16) for i in range(2)})
        cvb = {("v", i): V("cv%d" % i, 2048) for i in range(2)}
        cvb.update({("g", i): V("cg%d" % i, 2048) for i in range(2)})
        glb = [V("gl%d" % i, 2048) for i in range(2)]
        scT = V("scT", 88 * 128, F32, "p (j c) -> p j c", c=32)
        stg = [V("stg%d" % i, 2048) for i in range(2)]

        w_in_v = w_in.rearrange("(kc p) n -> p kc n", p=128)
        w_out_v = w_out.rearrange("(kc p) n -> p kc n", p=128)
        w_up_v = w_up.rearrange("(kc p) n -> p kc n", p=128)
        w_down_v = w_down.rearrange("(kc p) n -> p kc n", p=128)

        def wplan():
            items = []
            for _ in range(2):
                items += [("in", cb) for cb in CB_ORDER_KV]
            for _ in range(3):
                items += [("in", cb) for cb in CB_ORDER_FULL]
                items += [("out", cb) for cb in range(4)]
                for hf in range(2):
                    items += [("up", hf, g) for g in range(11)]
                    items += [("down", hf, nb, q) for nb in range(4) for q in range(2)]
            return items

        WITEMS = wplan()
        wstate = {"issued": 0, "used": 0}

        wsc_trk = {}

        def w_index(it):
            if it[0] == "in":
                return it[1], 8192
            if it[0] == "out":
                return 22 + it[1], 8192
            if it[0] == "up":
                return 26 + it[1] * 11 + it[2], 8192
            return 48 + it[1] * 8 + it[2] * 2 + it[3], 5632

        DKH = [(0, 8), (8, 11)]

        w_first = {}

        def w_issue_half(i, h):
            it = WITEMS[i]
            k = hb(i, h)
            idx, n = w_index(it)
            if (i, 0) not in w_first and (i, 1) not in w_first:
                if it in wsc_trk:
                    w_first[(i, 0)] = w_first[(i, 1)] = None
                else:
                    defer = (12 <= i < 76) and ((i - 12) % 3 == 2)
                    w_first[(i, 0)] = w_first[(i, 1)] = not defer
                    if not defer:
                        wsc_trk[it] = [K.trk("wsc%d_%d" % (idx, hh)) for hh in range(2)]
            first = w_first[(i, h)]
            if it[0] == "down":
                nk = DKH[h][1] - DKH[h][0]
                f0, ln = DKH[h][0] * 512, nk * 512
            else:
                nk = 8
                f0, ln = h * 4096, 4096
            if first is None:
                K.dma("pool", HBf[k][:, 0:ln], wsc[idx][:, f0:f0 + ln], S_HB[k], "load",
                      extra_reads=[wsc_trk[it][h]])
                return
            ks = slice(8 * h, 8 * h + 8)
            if it[0] == "in":
                K.dma("pool", HBv[k], w_in_v[:, ks, it[1] * 512:(it[1] + 1) * 512], S_HB[k], "load")
            elif it[0] == "out":
                K.dma("pool", HBv[k], w_out_v[:, ks, it[1] * 512:(it[1] + 1) * 512], S_HB[k], "load")
            elif it[0] == "up":
                j0 = it[1] * 22 + 2 * it[2]
                K.dma("pool", HBv[k][:, :, 0:256], w_up_v[:, ks, j0 * 128:j0 * 128 + 256], S_HB[k], "load")
                K.dma("pool", HBv[k][:, :, 256:512], w_up_v[:, ks, DFF + j0 * 128:DFF + j0 * 128 + 256],
                      S_HB[k], "load")
            else:
                _, hf, nb, q = it
                k0 = hf * 22 + q * 11
                K.dma("pool", HBv[k][:, 0:nk, :],
                      w_down_v[:, k0 + DKH[h][0]:k0 + DKH[h][1], nb * 512:(nb + 1) * 512], S_HB[k], "load")
            if first:
                K.dma("sp", wsc[idx][:, f0:f0 + ln], HBf[k][:, 0:ln], S_HBst[k], "store",
                      extra_writes=[wsc_trk[it][h]])

        whalf = {"next": 0}

        def w_issue_upto(code):
            while whalf["next"] <= min(code, 2 * len(WITEMS) - 1):
                c = whalf["next"]
                w_issue_half(c // 2, c % 2)
                whalf["next"] += 1

        def w_acquire(key):
            i = wstate["used"]
            assert WITEMS[i] == key, (WITEMS[i], key)
            w_issue_upto(2 * i + NHB - 1)
            wstate["used"] += 1
            wstate["cur"] = i
            return i

        def w_half_done(h):
            i = wstate["cur"]
            w_issue_upto(2 * i + h + NHB)

        K.dma("sp", consts, consts_in, S_const, "load")
        K.dma("sp", V("gT", 128), gT_in, S_const, "load")
        K.dma("sp", V("convtab", 88 * 16), convtab_in, S_const, "load")
        S_identb = K.slot("identb", [T["const"]])
        K.dma("pool", identb, consts_in[:, 0:128], S_identb, "load")
        K.wait_slot_all(S_const)
        K.wait_slot_all(S_identb)
        K.op("dve", lambda e: e.memset(V("hist", 88 * 8), 0.0), writes=[T["hist"]])
        K.op("dve", lambda e: e.memset(V("R", 8192), 0.0), writes=[T["R"]])
        K.op("dve", lambda e: e.memset(V("Rbf", 4096, BF16), 0.0), writes=[T["Rbf"]])

        sm_idx = [0]

        def sm_col():
            i = sm_idx[0] % 60
            sm_idx[0] += 1
            return small[:, i:i + 1]

        xb_i = [0]
        tr_i = [0]
        sg_i = [0]
        bank_i = [0]

        def next_bank():
            b = bank_i[0] % 8
            bank_i[0] += 1
            return b

        junkA = V("qkT", 4096, BF16)

        def stats(src_f32, src_trks, junk, junk_trk, ti):
            ss = sm_col()
            rs = sm_col()
            st = T["small%d" % (ti % 4)]
            K.op("act", lambda e: e.activation(out=junk, in_=src_f32, func=AF.Square, accum_out=ss),
                 reads=src_trks, writes=[junk_trk, st])
            K.op("act", lambda e: e.activation(out=rs, in_=ss, func=AF.Sqrt, scale=1.0 / D, bias=EPS),
                 reads=[st], writes=[st])
            K.op("dve", lambda e: e.reciprocal(out=rs, in_=rs), reads=[st], writes=[st])
            return rs, st

        def to_T(src_f32, src_trks, rs, st, tcol, gidx):
            i = xb_i[0] % 2
            xb_i[0] += 1
            xbt = T["xb%d" % i]
            K.op("act", lambda e: e.activation(out=xb[i], in_=src_f32, func=AF.Copy, scale=rs),
                 reads=list(src_trks) + [st], writes=[xbt])
            pb = 4 + 2 * (tr_i[0] % 2)
            tr_i[0] += 1
            pv = PSb(pb, 2)

            def tr(e):
                for kc in range(16):
                    ins = e.transpose(out=pv[:, kc * 128:(kc + 1) * 128], in_=xb[i][:, kc * 128:(kc + 1) * 128],
                                      identity=identb)
                return ins
            K.op("pe", tr, reads=[xbt], writes=[PST[pb], PST[pb + 1]])
            dst = hT[:, :, tcol * 128:(tcol + 1) * 128]
            src = pv.rearrange("p (k n) -> p k n", k=16)
            gb_ = bc(gT[:, gidx, :], 1, 128)
            K.op("dve", lambda e: e.tensor_tensor(out=dst, in0=src, in1=gb_, op=ALU.mult),
                 reads=[PST[pb], PST[pb + 1]], writes=[hTT[tcol]])

        def mm_half(wslot, h, tcol, bank):
            def f(e):
                for kc in range(8 * h, 8 * h + 8):
                    ins = e.matmul(PSf(bank), lhsT=hT[:, kc, tcol * 128:(tcol + 1) * 128],
                                   rhs=HBv[hb(wslot, h)][:, kc - 8 * h, :], start=(kc == 0), stop=(kc == 15))
                return ins
            K.op("pe", f, reads=[hTT[tcol], HBT[hb(wslot, h)]], writes=[PST[bank]])

        item_par = [0]

        def mm_item(wslot, ntl):
            base = 4 * (item_par[0] % 2)
            item_par[0] += 1
            banks = [base + i for i in range(ntl)]
            for h in range(2):
                for i in range(ntl):
                    mm_half(wslot, h, i, banks[i])
                w_half_done(h)
            return banks

        def rotary(bank, slot_i, heads0, dec_idx, dst, dst_trk, cs_slot):
            pv = PSf(bank).rearrange("p (h d) -> p h d", h=4)
            dq = bc(dec[:, dec_idx, heads0:heads0 + 4], 1, 128)
            K.op("dve", lambda e: e.tensor_tensor(out=rot, in0=pv, in1=dq, op=ALU.mult),
                 reads=[PST[bank]], writes=[T["rot"]])
            cosb = bc(cst[:, cs_slot, 0:64], 0, 4)
            sinb = bc(cst[:, cs_slot, 64:128], 0, 4)
            x1 = rot[:, :, 0:64]
            x2 = rot[:, :, 64:128]
            dv = dst.rearrange("p (h d) -> p h d", h=8)[:, heads0:heads0 + 4, :]
            K.op("dve", lambda e: e.tensor_tensor(out=t1, in0=x1, in1=cosb, op=ALU.mult), reads=[T["rot"], T["cs"]],
                 writes=[T["t1"]])
            K.op("dve", lambda e: e.tensor_tensor(out=t2, in0=x2, in1=sinb, op=ALU.mult), reads=[T["rot"], T["cs"]],
                 writes=[T["t2"]])
            K.op("dve", lambda e: e.tensor_tensor(out=dv[:, :, 0:64], in0=t1, in1=t2, op=ALU.subtract),
                 reads=[T["t1"], T["t2"]], writes=[dst_trk])
            K.op("dve", lambda e: e.tensor_tensor(out=t1, in0=x2, in1=cosb, op=ALU.mult), reads=[T["rot"]],
                 writes=[T["t1"]])
            K.op("dve", lambda e: e.tensor_tensor(out=t2, in0=x1, in1=sinb, op=ALU.mult), reads=[T["rot"]],
                 writes=[T["t2"]])
            K.op("dve", lambda e: e.tensor_tensor(out=dv[:, :, 64:128], in0=t1, in1=t2, op=ALU.add),
                 reads=[T["t1"], T["t2"]], writes=[dst_trk])

        def load_gain(idx):
            K.dma("sp", GB, gvec[idx].partition_broadcast(128), S_GB, "load")

        def kpp_prep(t):
            gcb = bc(dec[:, 4, :], 1, 128)
            kv = Xkb[t].rearrange("p (h d) -> p h d", h=8)
            K.op("dve", lambda e: e.tensor_tensor(out=kpp, in0=kv, in1=gcb, op=ALU.mult), reads=[Xk[t]],
                 writes=[T["kpp"]])

        def state_update_prompt(t, prep=True, mid=None):
            if prep:
                kpp_prep(t)

            def f(e):
                for h in range(8):
                    ins = e.matmul(ps[:, 2048 + h * 256:2048 + (h + 1) * 256], lhsT=kpp[:, h, :],
                                   rhs=Xvb[t][:, h * 256:(h + 1) * 256], start=True, stop=True)
                return ins
            K.op("pe", f, reads=[T["kpp"], Xv[t]], writes=PST[4:8])
            if mid is not None:
                mid()
            for h in range(8):
                K.op("dve", lambda e, h=h: e.scalar_tensor_tensor(
                    out=Rf32[:, h, :], in0=Rf32[:, h, :], scalar=GC_P[h], in1=ps[:, 2048 + h * 256:2048 + (h + 1) * 256],
                    op0=ALU.mult, op1=ALU.add), reads=[PST[4 + h // 2]], writes=[T["R"]])
            K.op("act", lambda e: e.copy(out=Rbf, in_=Rf32), reads=[T["R"]], writes=[T["Rbf"]])

        def qk_transposes(t):
            pv = PSb(4, 2).rearrange("p (c n) -> p c n", c=16)

            def f(e):
                for h in range(8):
                    e.transpose(out=pv[:, h, :], in_=Xqb[t][:, h * 128:(h + 1) * 128], identity=identb)
                for h in range(8):
                    ins = e.transpose(out=pv[:, 8 + h, :], in_=Xkb[t][:, h * 128:(h + 1) * 128], identity=identb)
                return ins
            K.op("pe", f, reads=[Xq[t], Xk[t]], writes=PST[4:6])
            K.op("dve", lambda e: e.tensor_copy(out=qkT, in_=pv), reads=PST[4:6], writes=[T["qkT"]])

        def scores(t, mask_idx):
            pv = PSf(6, 2).rearrange("p (h n) -> p h n", h=8)

            def f(e):
                for h in range(8):
                    ins = e.matmul(pv[:, h, :], lhsT=qkT[:, 8 + h, :], rhs=qkT[:, h, :], start=True, stop=True)
                return ins
            K.op("pe", f, reads=[T["qkT"]], writes=PST[6:8])
            mb_ = bc(cmask[mask_idx], 0, 8)
            K.op("dve", lambda e: e.tensor_tensor(out=STm, in0=pv, in1=mb_, op=ALU.mult), reads=PST[6:8],
                 writes=[T["STm"]])

        def o_evac(t):
            for b in range(4):
                K.op("act", lambda e, b=b: e.copy(out=Xf[t][:, b * 512:(b + 1) * 512], in_=PSf(b)),
                     reads=[PST[b]], writes=XT[t])

        def groupnorm_X(t):
            for h in range(8):
                K.op("dve", lambda e, h=h: e.bn_stats(out=st6[:, h, :], in_=Xf[t][:, h * 256:(h + 1) * 256]),
                     reads=XT[t], writes=[T["gnst"]])
            for h in range(8):
                K.op("dve", lambda e, h=h: e.bn_aggr(out=mv[:, h, :], in_=st6[:, h, :]), reads=[T["gnst"]],
                     writes=[T["gnst"]])
            K.op("act", lambda e: e.activation(out=rs8, in_=mv[:, :, 1], func=AF.Sqrt, scale=1.0, bias=EPS),
                 reads=[T["gnst"]], writes=[T["gnst"]])
            K.op("dve", lambda e: e.reciprocal(out=rs8, in_=rs8), reads=[T["gnst"]], writes=[T["gnst"]])
            K.op("dve", lambda e: e.scalar_tensor_tensor(out=nm8, in0=mv[:, :, 0], scalar=-1.0, in1=rs8,
                                                         op0=ALU.mult, op1=ALU.mult),
                 reads=[T["gnst"]], writes=[T["gnst"]])
            for h in range(8):
                K.op("act", lambda e, h=h: e.activation(out=Xf[t][:, h * 256:(h + 1) * 256],
                                                        in_=Xf[t][:, h * 256:(h + 1) * 256], func=AF.Identity,
                                                        scale=rs8[:, h:h + 1], bias=nm8[:, h:h + 1]),
                     reads=XT[t] + [T["gnst"]], writes=XT[t])
            K.op("dve", lambda e: e.tensor_tensor(out=Xf[t], in0=Xf[t], in1=GB, op=ALU.mult),
                 reads=XT[t] + [T["GB"]], writes=XT[t])

        def groupnorm_to_X(t):
            o_evac(t)
            groupnorm_X(t)

        def retention_prompt_a(t):
            qk_transposes(t)
            scores(t, 0)
            kpp_prep(t)

        def retention_prompt_b(t):
            def f(e):
                for h in range(8):
                    o = ps[:, h * 256:(h + 1) * 256]
                    e.matmul(o, lhsT=STm[:, h, :], rhs=Xvb[t][:, h * 256:(h + 1) * 256], start=True, stop=False)
                    ins = e.matmul(o, lhsT=qkT[:, h, :], rhs=Rbf[:, h, :], start=False, stop=True)
                return ins
            K.op("pe", f, reads=[T["STm"], Xv[t], T["qkT"], T["Rbf"]], writes=PST[0:4])
            state_update_prompt(t, prep=False, mid=lambda: o_evac(t))

        def retention_sample(t):
            qk_transposes(t)
            scores(t, 1)
            K.op("dve", lambda e: e.memset(ps[:, 0:2048], 0.0), writes=PST[0:4])
            for i in range(2):
                K.op("dve", lambda e, i=i: e.memset(qTm[i], 0.0), writes=[T["qTm%d" % i]])
            gcb = bc(dec[:, 5, :], 1, 128)
            kv = Xkb[t].rearrange("p (h d) -> p h d", h=8)
            sr_v = sr.rearrange("s h d v -> s d h v")
            nr_v = nr_s.rearrange("s h d v -> s d h v")
            def inherit(dst, src):
                n = 0
                evs = list(src.r.values()) + ([src.w] if src.w is not None else [])
                for ev in evs:
                    dst.r[("inh", id(src), n)] = ev
                    n += 1

            for i in range(8):
                inherit(YH[i], YT[i // 2])
            def prep(s):
                b2 = s % 2
                cols = slice(s * 8, (s + 1) * 8)
                K.op("dve", lambda e, b2=b2, cols=cols: e.tensor_copy(out=qTm[b2][:, :, cols], in_=qkT[:, 0:8, cols]),
                     reads=[T["qkT"]], writes=[T["qTm%d" % b2]])
                K.op("dve", lambda e, b2=b2, s=s: e.scalar_tensor_tensor(
                    out=kppm[b2], in0=kv, scalar=rowmask[:, s:s + 1], in1=gcb, op0=ALU.mult, op1=ALU.mult),
                    reads=[Xk[t]], writes=[T["kppm%d" % b2]])

            def rload(k):
                s, hh = k // 2, k % 2
                K.dma("sp", YHv[k % 8], sr_v[s, :, 4 * hh:4 * hh + 4, :], S_YH[k % 8], "load")

            for k in range(8):
                rload(k)
            prep(0)
            for s in range(16):
                b2 = s % 2
                cols = slice(s * 8, (s + 1) * 8)
                if s + 1 < 16:
                    prep(s + 1)
                for hh in range(2):
                    bi = (2 * s + hh) % 8
                    b = hh
                    rfv = YHv[bi]
                    K.op("act", lambda e, b=b, rfv=rfv: e.copy(out=RbS[b], in_=rfv), reads=[YH[bi]],
                         writes=[T["Rb%d" % b]])

                    def f(e, b=b, b2=b2, hh=hh):
                        for hl in range(4):
                            h = 4 * hh + hl
                            ins = e.matmul(ps[:, h * 256:(h + 1) * 256], lhsT=qTm[b2][:, h, :], rhs=RbS[b][:, hl, :],
                                           start=False, stop=False, skip_group_check=True)
                        return ins
                    K.op("pe", f, reads=[T["qTm%d" % b2], T["Rb%d" % b]], writes=PST[2 * hh:2 * hh + 2])
                    pb = 4 + 2 * hh

                    def f2(e, b2=b2, hh=hh, pb=pb):
                        for hl in range(4):
                            h = 4 * hh + hl
                            ins = e.matmul(ps[:, pb * 512 + hl * 256:pb * 512 + (hl + 1) * 256], lhsT=kppm[b2][:, h, :],
                                           rhs=Xvb[t][:, h * 256:(h + 1) * 256], start=True, stop=True)
                        return ins
                    K.op("pe", f2, reads=[T["kppm%d" % b2], Xv[t]], writes=PST[pb:pb + 2])
                for hh in range(2):
                    bi = (2 * s + hh) % 8
                    rfv = YHv[bi]
                    pb = 4 + 2 * hh
                    for hl in range(4):
                        h = 4 * hh + hl
                        K.op("dve", lambda e, rfv=rfv, hl=hl, h=h, pb=pb: e.scalar_tensor_tensor(
                            out=rfv[:, hl, :], in0=rfv[:, hl, :], scalar=GC_S[h],
                            in1=ps[:, pb * 512 + hl * 256:pb * 512 + (hl + 1) * 256], op0=ALU.mult, op1=ALU.add),
                            reads=[PST[pb + hl // 2]], writes=[YH[bi]])
                    K.dma("pool", nr_v[s, :, 4 * hh:4 * hh + 4, :], rfv, S_YHst[bi], "store", is_output=True)
                    if 2 * s + hh + 8 < 32:
                        rload(2 * s + hh + 8)
                K.op("dve", lambda e, b2=b2, cols=cols: e.memset(qTm[b2][:, :, cols], 0.0),
                     writes=[T["qTm%d" % b2]])
            for i in range(8):
                inherit(YT[i // 2], YH[i])

            def f3(e):
                for h in range(8):
                    ins = e.matmul(ps[:, h * 256:(h + 1) * 256], lhsT=STm[:, h, :], rhs=Xvb[t][:, h * 256:(h + 1) * 256],
                                   start=False, stop=True, skip_group_check=True)
                return ins
            K.op("pe", f3, reads=[T["STm"], Xv[t]], writes=PST[0:4])
            groupnorm_to_X(t)

        def pooling(t, kind, prev_ap, prev_trk, band_cur_idx):
            pz = PSf(4, 2).rearrange("p (c n) -> p c n", c=8)

            def f(e):
                for cc in range(8):
                    g = cc // 2
                    if kind == "first":
                        ins = e.matmul(pz[:, cc, :], lhsT=Yub[t][:, cc * 128:(cc + 1) * 128],
                                       rhs=bands[:, band_cur_idx, g, :], start=True, stop=True)
                    elif kind == "prompt":
                        e.matmul(pz[:, cc, :], lhsT=Yub[t][:, cc * 128:(cc + 1) * 128],
                                 rhs=bands[:, band_cur_idx, g, :], start=True, stop=False)
                        ins = e.matmul(pz[:, cc, :], lhsT=prev_ap[:, cc * 128:(cc + 1) * 128],
                                       rhs=bands[:, 2, g, :], start=False, stop=True)
                    else:
                        e.matmul(pz[:, cc, :], lhsT=Yub[t][:, cc * 128:(cc + 1) * 128],
                                 rhs=bands[:, 3, g, :], start=True, stop=False)
                        e.matmul(pz[:, cc, :], lhsT=spbf[0:120, 0, cc * 128:(cc + 1) * 128],
                                 rhs=bandh[0:120, 0, g, :], start=False, stop=False)
                        ins = e.matmul(pz[:, cc, :], lhsT=spbf[0:120, 1, cc * 128:(cc + 1) * 128],
                                       rhs=bandh[0:120, 1, g, :], start=False, stop=True)
                return ins
            rd = [YT[t], T["bands"]]
            if kind == "prompt":
                rd.append(prev_trk)
            if kind == "sample":
                rd.append(T["spbf"])
            K.op("pe", f, reads=rd, writes=PST[4:6])
            K.op("act", lambda e: e.copy(out=zT, in_=pz), reads=PST[4:6], writes=[T["zT"]])

            def f2(e):
                for g in range(4):
                    for kc in range(2):
                        ins = e.matmul(PSf(g), lhsT=zT[:, 2 * g + kc, :], rhs=wpool[:, 2 * g + kc, :],
                                       start=(kc == 0), stop=(kc == 1))
                return ins
            K.op("pe", f2, reads=[T["zT"], T["wpool"]], writes=PST[0:4])
            K.op("act", lambda e: e.copy(out=Yf[t], in_=ps[:, 0:2048]), reads=PST[0:4], writes=[YT[t]])

        def load_phaseA_consts():
            K.dma("pool", V("bands", 4096, BF16), bands_in, S_bands, "load",
                  extra_reads=[])
            K.dma("pool", V("bandh", 2048, BF16)[0:120, :], bandh_in, S_bands, "load")
            load_gain(0)
            wv = w_pool.rearrange("g (kc p) n -> p g kc n", p=128)
            for g in range(4):
                i = sg_i[0] % 2
                sg_i[0] += 1
                for kc in range(2):
                    K.dma("sp", sg[i], wv[:, g, kc, :], S_sg[i], "load")
                    K.op("dve", lambda e, g=g, kc=kc, i=i: e.tensor_tensor(
                        out=wpool[:, 2 * g + kc, :], in0=sg[i], in1=GB[:, g * 512:(g + 1) * 512], op=ALU.mult),
                        reads=[T["sg%d" % i], T["GB"]], writes=[T["wpool"]])
            K.op("act", lambda e: e.copy(out=Rbf, in_=Rf32), reads=[T["R"]], writes=[T["Rbf"]])

        def alias_fence(new_names, old_names):
            evs_w = []
            for n in old_names:
                t = T[n]
                for nn in new_names:
                    tt = T[nn]
                    if t.w is not None:
                        tt.r[id(t.w[0]) + 1] = t.w
                    for k, ev in t.r.items():
                        tt.r[k + 2] = ev

        def kv_pass(tile0, ntiles):
            K.dma("sp", cst[:, 0:ntiles, :], cs_in[tile0:tile0 + ntiles].rearrange("t p c -> p t c"),
                  S_cs, "load")
            for t in range(ntiles):
                K.dma("sp", Xf[t], xall[(tile0 + t) * 128:(tile0 + t + 1) * 128, :], S_X[t], "load")
            rr = {0: stats(Xf[0], XT[0], junkA, T["qkT"], 0)}
            for t in range(ntiles):
                if t + 1 < ntiles:
                    rr[t + 1] = stats(Xf[t + 1], XT[t + 1], junkA, T["qkT"], t + 1)
                to_T(Xf[t], XT[t], rr[t][0], rr[t][1], t, 0)
            for cb in CB_ORDER_KV:
                wslot = w_acquire(("in", cb))
                banks = mm_item(wslot, ntiles)
                for t in range(ntiles):
                    bank = banks[t]
                    if cb in (4, 5):
                        rotary(bank, None, 4 * (cb - 4), 1, Xkb[t], Xk[t], t)
                    else:
                        K.op("act", lambda e, t=t, cb=cb, bank=bank: e.copy(
                            out=Xvb[t][:, (cb - 6) * 512:(cb - 5) * 512], in_=PSf(bank)),
                            reads=[PST[bank]], writes=[Xv[t]])
            for t in range(ntiles):
                state_update_prompt(t)

        pending_conv = []

        ssf = V("gnst", 64, F32, "p (t c) -> p t c", t=4, boff=192 + 128)

        def full_pass(pi, tiles):
            nt = len(tiles)
            N = nt * 128
            alias_fence(PHA, PHB)
            for i, tl in enumerate(tiles):
                K.dma("sp", Xf[i], tl["xsrc"], S_X[i], "load")
            for i, tl in enumerate(tiles):
                K.dma("sp", cst[:, i, :], cs_in[tl["cs"]], S_cs, "load")
            rr = {0: stats(Xf[0], XT[0], junkA, T["qkT"], 0)}
            for i in range(nt):
                if i + 1 < nt:
                    rr[i + 1] = stats(Xf[i + 1], XT[i + 1], junkA, T["qkT"], i + 1)
                to_T(Xf[i], XT[i], rr[i][0], rr[i][1], i, 0)
            if pending_conv:
                for a_ in pending_conv:
                    conv_state_out(*a_)
                del pending_conv[:]
                alias_fence(PHA, PHB)
            load_phaseA_consts()
            ck("p%d_ph1" % pi)
            for cb in CB_ORDER_FULL:
                wslot = w_acquire(("in", cb))
                banks = mm_item(wslot, nt)
                for i, tl in enumerate(tiles):
                    bank = banks[i]
                    smp = tl["kind"] == "sample"
                    if cb in (2, 3):
                        rotary(bank, None, 4 * (cb - 2), 2 if smp else 0, Xqb[i], Xq[i], i)
                    elif cb in (4, 5):
                        rotary(bank, None, 4 * (cb - 4), 3 if smp else 1, Xkb[i], Xk[i], i)
                    elif 6 <= cb <= 9:
                        K.op("act", lambda e, i=i, cb=cb, bank=bank: e.copy(
                            out=Xvb[i][:, (cb - 6) * 512:(cb - 5) * 512], in_=PSf(bank)),
                            reads=[PST[bank]], writes=[Xv[i]])
                    elif cb in (0, 1):
                        K.op("act", lambda e, i=i, cb=cb, bank=bank: e.copy(
                            out=Yub[i][:, cb * 512:(cb + 1) * 512], in_=PSf(bank)),
                            reads=[PST[bank]], writes=[YT[i]])
                        if tl.get("pool_out") is not None and debug != "nopoolout" and not (debug or "").endswith("_x"):
                            j = sg_i[0] % 2
                            sg_i[0] += 1
                            K.op("dve", lambda e, j=j, bank=bank: e.tensor_copy(out=sg[j], in_=PSf(bank)),
                                 reads=[PST[bank]], writes=[T["sg%d" % j]])
                            for (dst, p0, p1) in tl["pool_out"](cb):
                                K.dma("sp", dst, sg[j][p0:p1, :], S_po[j], "store", is_output=True)
                    elif 10 <= cb <= 13:
                        j = sg_i[0] % 2
                        sg_i[0] += 1
                        c0 = (cb - 10) * 512
                        K.op("act", lambda e, j=j, bank=bank: e.activation(out=sg[j], in_=PSf(bank), func=AF.Silu),
                             reads=[PST[bank]], writes=[T["sg%d" % j]])
                        K.op("dve", lambda e, i=i, j=j, c0=c0: e.tensor_tensor(
                            out=Xf[i][:, c0:c0 + 512], in0=Xf[i][:, c0:c0 + 512], in1=sg[j], op=ALU.mult),
                            reads=XT[i] + [T["sg%d" % j]], writes=XT[i])
                    elif 14 <= cb <= 17:
                        j = sg_i[0] % 2
                        sg_i[0] += 1
                        c0 = (cb - 14) * 512
                        K.op("act", lambda e, j=j, bank=bank: e.activation(out=sg[j], in_=PSf(bank),
                                                                           func=AF.Sigmoid),
                             reads=[PST[bank]], writes=[T["sg%d" % j]])
                        K.op("dve", lambda e, i=i, j=j, c0=c0: e.tensor_tensor(
                            out=Yf[i][:, c0:c0 + 512], in0=Yf[i][:, c0:c0 + 512], in1=sg[j], op=ALU.mult),
                            reads=[YT[i], T["sg%d" % j]], writes=[YT[i]])
                    else:
                        j = sg_i[0] % 2
                        sg_i[0] += 1
                        c0 = (cb - 18) * 512
                        K.op("act", lambda e, j=j, bank=bank: e.activation(out=sg[j], in_=PSf(bank),
                                                                           func=AF.Sigmoid),
                             reads=[PST[bank]], writes=[T["sg%d" % j]])
                        K.op("dve", lambda e, i=i, j=j, c0=c0: e.tensor_tensor(
                            out=sg[j], in0=Xf[i][:, c0:c0 + 512], in1=sg[j], op=ALU.mult),
                            reads=XT[i] + [T["sg%d" % j]], writes=[T["sg%d" % j]])
                        K.op("dve", lambda e, i=i, j=j, c0=c0: e.tensor_tensor(
                            out=Yf[i][:, c0:c0 + 512], in0=Yf[i][:, c0:c0 + 512], in1=sg[j], op=ALU.add),
                            reads=[YT[i], T["sg%d" % j]], writes=[YT[i]])
                if cb == 9:
                    load_gain(1)
                    pend = None
                    for i, tl in enumerate(tiles):
                        if tl["kind"] == "sample":
                            retention_sample(i)
                        else:
                            retention_prompt_a(i)
                            if pend is not None:
                                groupnorm_X(pend)
                            retention_prompt_b(i)
                            pend = i
                    if pend is not None:
                        groupnorm_X(pend)
                    if tiles[-1].get("ret_out"):
                        K.dma("sp", nr_p.rearrange("h d v -> d h v"), Rf32, S_R, "store", is_output=True)
                    ck("p%d_ret" % pi)
                if cb == 1:
                    ck("p%d_cb1" % pi)
                    last = nt - 1
                    if any(tl["kind"] == "sample" for tl in tiles):
                        K.dma("pool", spbf[0:120, :, :], sp_in, S_spbf, "load")
                    tmpU = V("STm", 2048, BF16)
                    K.op("dve", lambda e: e.tensor_copy(out=tmpU, in_=Yub[last]), reads=[YT[last]],
                         writes=[T["STm"]])
                    for i in range(nt - 1, -1, -1):
                        tl = tiles[i]
                        if tl["kind"] == "sample":
                            pooling(i, "sample", None, None, 3)
                        elif tl["prev"] == "none":
                            pooling(i, "first", None, None, tl["band"])
                        elif tl["prev"] == "carry":
                            pooling(i, "prompt", carryU, T["carryU"], tl["band"])
                        else:
                            pooling(i, "prompt", Yub[tl["prev"]], YT[tl["prev"]], tl["band"])
                    K.op("dve", lambda e: e.tensor_copy(out=carryU, in_=tmpU), reads=[T["STm"]],
                         writes=[T["carryU"]])
                    ck("p%d_pool" % pi)
            ck("p%d_ph2" % pi)
            for i, tl in enumerate(tiles):
                j = xb_i[0] % 2
                xb_i[0] += 1
                K.op("act", lambda e, i=i, j=j: e.copy(out=xb[j], in_=Yf[i]), reads=[YT[i]], writes=[T["xb%d" % j]])
                pb = 4 + 2 * (tr_i[0] % 2)
                tr_i[0] += 1
                pv = PSb(pb, 2)

                def tr(e, j=j, pv=pv):
                    for kc in range(16):
                        ins = e.transpose(out=pv[:, kc * 128:(kc + 1) * 128], in_=xb[j][:, kc * 128:(kc + 1) * 128],
                                          identity=identb)
                    return ins
                K.op("pe", tr, reads=[T["xb%d" % j]], writes=[PST[pb], PST[pb + 1]])
                K.op("dve", lambda e, i=i, pv=pv: e.tensor_copy(out=hT[:, :, i * 128:(i + 1) * 128],
                                                                in_=pv.rearrange("p (k n) -> p k n", k=16)),
                     reads=[PST[pb], PST[pb + 1]], writes=[hTT[i]])
                K.dma("sp", Xf[i], tl["xsrc"], S_X[i], "load")
            ck("p%d_merge" % pi)
            load_gain(2)
            ssp = V("gnst", 64, F32, "p (t c) -> p t c", t=4, boff=192 + 128)
            for cb in range(4):
                wslot = w_acquire(("out", cb))
                banks = mm_item(wslot, nt)
                for i, tl in enumerate(tiles):
                    bank = banks[i]
                    K.op("act", lambda e, i=i, cb=cb, bank=bank: e.activation(
                        out=junkA[:, 0:512], in_=PSf(bank), func=AF.Square, accum_out=ssp[:, i, cb:cb + 1]),
                        reads=[PST[bank]], writes=[T["qkT"], T["small%d" % i]])
                    K.op("dve", lambda e, i=i, cb=cb, bank=bank: e.tensor_tensor(
                        out=Yf[i][:, cb * 512:(cb + 1) * 512], in0=PSf(bank), in1=GB[:, cb * 512:(cb + 1) * 512],
                        op=ALU.mult), reads=[PST[bank], T["GB"]], writes=[YT[i]])
            def tail4(i):
                st = T["small%d" % i]
                ss = sm_col()
                rs = sm_col()
                K.op("dve", lambda e, i=i, ss=ss: e.tensor_reduce(out=ss, in_=ssp[:, i, :], axis=mybir.AxisListType.X,
                                                                  op=ALU.add), reads=[st], writes=[st])
                K.op("act", lambda e, ss=ss, rs=rs: e.activation(out=rs, in_=ss, func=AF.Sqrt, scale=1.0 / D,
                                                                 bias=EPS), reads=[st], writes=[st])
                K.op("dve", lambda e, rs=rs: e.reciprocal(out=rs, in_=rs), reads=[st], writes=[st])
                K.op("dve", lambda e, i=i, rs=rs: e.scalar_tensor_tensor(out=Xf[i], in0=Yf[i], scalar=rs, in1=Xf[i],
                                                                         op0=ALU.mult, op1=ALU.add),
                     reads=XT[i] + [YT[i], st], writes=XT[i])

            tail4(0)
            rr = {0: stats(Xf[0], XT[0], junkA, T["qkT"], 0)}
            for i in range(nt):
                if i + 1 < nt:
                    tail4(i + 1)
                    rr[i + 1] = stats(Xf[i + 1], XT[i + 1], junkA, T["qkT"], i + 1)
                to_T(Xf[i], XT[i], rr[i][0], rr[i][1], i, 1)
            ck("p%d_ph4" % pi)
            alias_fence(PHB, PHA)
            has_sample = tiles[0]["kind"] == "sample"
            c0p = 128 if has_sample else 0
            Np = N - c0p
            if has_sample:
                for r in range(22):
                    j = r % 2
                    K.dma("sp", stg[j][0:32, :], sc[:, r * 512:(r + 1) * 512], S_stg[j], "load")
                    pb = 4 + 2 * (r % 2)

                    def f(e, j=j, pb=pb):
                        for c in range(4):
                            ins = e.transpose(out=ps[:, pb * 512 + c * 32:pb * 512 + (c + 1) * 32],
                                              in_=stg[j][0:32, c * 128:(c + 1) * 128], identity=identf[0:32, 0:32])
                        return ins
                    K.op("pe", f, reads=[T["stg%d" % j]], writes=[PST[pb]])
                    K.op("dve", lambda e, r=r, pb=pb: e.tensor_copy(
                        out=scT[:, r * 4:(r + 1) * 4, :],
                        in_=ps[:, pb * 512:pb * 512 + 128].rearrange("p (c n) -> p c n", c=4)),
                        reads=[PST[pb]], writes=[T["scT"]])
            ck("p%d_scT" % pi)
            for hf in range(2):
                up_i = 0
                for g in range(11):
                    wslot = w_acquire(("up", hf, g))
                    base = 4 * (item_par[0] % 2)
                    item_par[0] += 1
                    for h in range(2):
                        for jj in range(2):
                            for (kind, pb, woff) in (("v", base + 2 * jj, 0), ("g", base + 2 * jj + 1, 256)):
                                def f(e, pb=pb, woff=woff, jj=jj, wslot=wslot, h=h):
                                    for kc in range(8 * h, 8 * h + 8):
                                        ins = e.matmul(ps[:, pb * 512:pb * 512 + N],
                                                       lhsT=HBv[hb(wslot, h)][:, kc - 8 * h,
                                                                              woff + jj * 128:woff + (jj + 1) * 128],
                                                       rhs=hT[:, kc, 0:N], start=(kc == 0), stop=(kc == 15))
                                    return ins
                                K.op("pe", f, reads=[HBT[hb(wslot, h)]] + hTT[0:nt], writes=[PST[pb]])
                        w_half_done(h)
                    for jj in range(2):
                        jl = 2 * g + jj
                        j = hf * 22 + jl
                        pbv = base + 2 * jj
                        pbg = pbv + 1
                        rb = jj
                        for (kind, pb, woff, jidx) in (("v", pbv, 0, j), ("g", pbg, 256, NJ + j)):
                            ub = upb[(kind, rb)]
                            ut = T["up%s%d" % (kind, rb)]
                            cvv = cvb[(kind, rb)]
                            cvt = T["c%s%d" % (kind, rb)]
                            w0 = convtab[:, jidx, 0:1]
                            w1 = convtab[:, jidx, 1:2]
                            w2 = convtab[:, jidx, 2:3]
                            bb = convtab[:, jidx, 3:4]
                            K.op("act", lambda e, ub=ub, pb=pb: e.copy(out=ub[:, 2:2 + Np],
                                                                       in_=ps[:, pb * 512 + c0p:pb * 512 + N]),
                                 reads=[PST[pb]], writes=[ut])
                            K.op("dve", lambda e, ub=ub, jidx=jidx: e.tensor_copy(out=ub[:, 0:2], in_=hist[:, jidx, :]),
                                 reads=[T["hist"]], writes=[ut])
                            K.op("dve", lambda e, ub=ub, jidx=jidx: e.tensor_copy(out=hist[:, jidx, :],
                                                                                  in_=ub[:, Np:Np + 2]),
                                 reads=[ut], writes=[T["hist"]])
                            K.op("dve", lambda e, ub=ub, cvv=cvv, w2=w2, bb=bb: e.tensor_scalar(
                                out=cvv[:, c0p:N], in0=ub[:, 2:2 + Np], scalar1=w2, scalar2=bb, op0=ALU.mult,
                                op1=ALU.add), reads=[ut], writes=[cvt])
                            K.op("dve", lambda e, ub=ub, cvv=cvv, w1=w1: e.scalar_tensor_tensor(
                                out=cvv[:, c0p:N], in0=ub[:, 1:1 + Np], scalar=w1, in1=cvv[:, c0p:N], op0=ALU.mult,
                                op1=ALU.add), reads=[ut, cvt], writes=[cvt])
                            K.op("dve", lambda e, ub=ub, cvv=cvv, w0=w0: e.scalar_tensor_tensor(
                                out=cvv[:, c0p:N], in0=ub[:, 0:Np], scalar=w0, in1=cvv[:, c0p:N], op0=ALU.mult,
                                op1=ALU.add), reads=[ut, cvt], writes=[cvt])
                            if has_sample:
                                us = usb[(kind, rb)]
                                ust = T["us%s%d" % (kind, rb)]
                                K.op("act", lambda e, us=us, pb=pb: e.copy(
                                    out=us[:, :, 2:10],
                                    in_=ps[:, pb * 512:pb * 512 + 128].rearrange("p (s c) -> p s c", s=16)),
                                    reads=[PST[pb]], writes=[ust])
                                K.op("dve", lambda e, us=us, jidx=jidx: e.tensor_copy(
                                    out=us[:, :, 0:2], in_=scT[:, jidx, :].rearrange("p (s c) -> p s c", s=16)),
                                    reads=[T["scT"]], writes=[ust])
                                K.op("dve", lambda e, us=us, jidx=jidx: e.tensor_copy(
                                    out=scT[:, jidx, :].rearrange("p (s c) -> p s c", s=16), in_=us[:, :, 8:10]),
                                    reads=[ust], writes=[T["scT"]])
                                cs_ = cvv[:, 0:128].rearrange("p (s c) -> p s c", s=16)
                                K.op("dve", lambda e, us=us, cs_=cs_, w2=w2, bb=bb: e.tensor_scalar(
                                    out=cs_, in0=us[:, :, 2:10], scalar1=w2, scalar2=bb, op0=ALU.mult, op1=ALU.add),
                                    reads=[ust], writes=[cvt])
                                K.op("dve", lambda e, us=us, cs_=cs_, w1=w1: e.scalar_tensor_tensor(
                                    out=cs_, in0=us[:, :, 1:9], scalar=w1, in1=cs_, op0=ALU.mult, op1=ALU.add),
                                    reads=[ust, cvt], writes=[cvt])
                                K.op("dve", lambda e, us=us, cs_=cs_, w0=w0: e.scalar_tensor_tensor(
                                    out=cs_, in0=us[:, :, 0:8], scalar=w0, in1=cs_, op0=ALU.mult, op1=ALU.add),
                                    reads=[ust, cvt], writes=[cvt])
                        K.op("act", lambda e, rb=rb: e.activation(out=glb[rb][:, 0:N], in_=cvb[("g", rb)][:, 0:N],
                                                                  func=AF.Gelu_apprx_tanh),
                             reads=[T["cg%d" % rb]], writes=[T["gl%d" % rb]])
                        K.op("dve", lambda e, rb=rb, jl=jl: e.tensor_tensor(out=actT[:, jl, 0:N], in0=glb[rb][:, 0:N],
                                                                            in1=cvb[("v", rb)][:, 0:N], op=ALU.mult),
                             reads=[T["gl%d" % rb], T["cv%d" % rb]], writes=[actTT[jl]])
                ck("p%d_up%d" % (pi, hf))
                if hf == 1:
                    K.dma("sp", GBf, gvec[3].partition_broadcast(128), S_GBf, "load")
                out_tiles = [i for i, tl in enumerate(tiles) if tl.get("yout") is not None]
                for nb in range(4):
                    base = 0 if nb % 2 == 0 else 4
                    for q in range(2):
                        wslot = w_acquire(("down", hf, nb, q))
                        for h in range(2):
                            for i in out_tiles:
                                def f(e, i=i, q=q, wslot=wslot, base=base, h=h):
                                    for kk in range(DKH[h][0], DKH[h][1]):
                                        ins = e.matmul(PSf(base + i), lhsT=actT[:, q * 11 + kk, i * 128:(i + 1) * 128],
                                                       rhs=HBv[hb(wslot, h)][:, kk - DKH[h][0], :],
                                                       start=(q == 0 and kk == 0), stop=(q == 1 and kk == 10))
                                    return ins
                                K.op("pe", f, reads=actTT[q * 11 + DKH[h][0]:q * 11 + DKH[h][1]] + [HBT[hb(wslot, h)]],
                                     writes=[PST[base + i]])
                            w_half_done(h)
                    for i in out_tiles:
                        if hf == 0:
                            K.op("act", lambda e, i=i, nb=nb, base=base: e.copy(
                                out=Yf[i][:, nb * 512:(nb + 1) * 512], in_=PSf(base + i)),
                                reads=[PST[base + i]], writes=[YT[i]])
                        else:
                            ysl = Yf[i][:, nb * 512:(nb + 1) * 512]
                            K.op("dve", lambda e, ysl=ysl, i=i, base=base: e.tensor_tensor(
                                out=ysl, in0=ysl, in1=PSf(base + i), op=ALU.add),
                                reads=[PST[base + i], YT[i]], writes=[YT[i]])
                            K.op("act", lambda e, ysl=ysl, i=i, nb=nb: e.activation(
                                out=xb[0][:, 0:512], in_=ysl, func=AF.Square, accum_out=ssf[:, i, nb:nb + 1]),
                                reads=[YT[i]], writes=[T["xb0"], T["small%d" % i]])
                            K.op("dve", lambda e, ysl=ysl, nb=nb: e.tensor_tensor(
                                out=ysl, in0=ysl, in1=GBf[:, nb * 512:(nb + 1) * 512], op=ALU.mult),
                                reads=[YT[i], T["cv0"], T["cv1"], T["cg0"], T["cg1"]], writes=[YT[i]])
            ck("p%d_ffn" % pi)
            outs_ = [i for i, tl in enumerate(tiles) if tl.get("yout") is not None]
            for i in outs_:
                st = T["small%d" % i]
                ss = sm_col()
                rs = sm_col()
                K.op("dve", lambda e, i=i, ss=ss: e.tensor_reduce(out=ss, in_=ssf[:, i, :], axis=mybir.AxisListType.X,
                                                                  op=ALU.add), reads=[st], writes=[st])
                K.op("act", lambda e, ss=ss, rs=rs: e.activation(out=rs, in_=ss, func=AF.Sqrt, scale=1.0 / D,
                                                                 bias=EPS), reads=[st], writes=[st])
                K.op("dve", lambda e, rs=rs: e.reciprocal(out=rs, in_=rs), reads=[st], writes=[st])
                K.op("dve", lambda e, i=i, rs=rs: e.scalar_tensor_tensor(out=Yf[i], in0=Yf[i], scalar=rs, in1=Xf[i],
                                                                         op0=ALU.mult, op1=ALU.add),
                     reads=XT[i] + [YT[i], st], writes=[YT[i]])
                K.dma("sp", tiles[i]["yout"], Yf[i], S_Y[i], "store", is_output=True)
            ck("p%d_y" % pi)
            if has_sample:
                pending_conv.append((scT, 32, nc_s))
            if tiles[-1].get("conv_out"):
                conv_state_out(hist, 2, nc_p)

        def conv_state_out(src, width, dst):
            for r in range(22):
                j = r % 2
                pb = 4 + 2 * (r % 2)

                def f(e, r=r, pb=pb):
                    for c in range(4):
                        ins = e.transpose(out=ps[0:width, pb * 512 + c * 128:pb * 512 + (c + 1) * 128],
                                          in_=src[:, r * 4 + c, :], identity=identf)
                    return ins
                K.op("pe", f, reads=[T["scT"], T["hist"]], writes=[PST[pb]])
                K.op("dve", lambda e, j=j, pb=pb: e.tensor_copy(out=stg[j][0:width, :],
                                                                in_=ps[0:width, pb * 512:pb * 512 + 512]),
                     reads=[PST[pb]], writes=[T["stg%d" % j]])
                K.dma("sp", dst[:, r * 512:(r + 1) * 512], stg[j][0:width, :], S_stg[j], "store", is_output=True)

        def program_body():

            def np_s_out(cb):
                return [(np_s[:, 7:15, cb * 512:(cb + 1) * 512], 0, 128)]

            def np_p_out(cb):
                return [(np_p[:, cb * 512:(cb + 1) * 512], 113, 128)]

            def mtile(m):
                return dict(kind="prompt", xsrc=xall[(8 + m) * 128:(9 + m) * 128, :], cs=8 + m, band=1,
                            yout=y_main[m * 128:(m + 1) * 128, :])

            tS = dict(kind="sample", xsrc=xs, cs=16, band=3, prev=None, yout=y_s, pool_out=np_s_out)
            tC = dict(kind="prompt", xsrc=xall[7 * 128:8 * 128, :], cs=7, band=1, prev="none", yout=None)
            m = [mtile(i) for i in range(8)]
            m[0]["band"] = 0
            m[0]["prev"] = 1
            m[1]["prev"] = 2
            m[2]["prev"] = "carry"
            m[3]["prev"] = 0
            m[4]["prev"] = 1
            m[5]["prev"] = "carry"
            m[6]["prev"] = 0
            m[7]["prev"] = 1
            m[7]["pool_out"] = np_p_out
            m[7]["ret_out"] = True
            m[7]["conv_out"] = True
            K.dma("sp", np_s[:, 0:7, :], spraw[:, 8:15, :], S_misc, "load", is_output=True)
            ck("d2d")

            full_pass(0, [tS, tC, m[0], m[1]])
            full_pass(1, [m[2], m[3], m[4]])
            full_pass(2, [m[5], m[6], m[7]])


        try:
            ck("setup")
            kv_pass(0, 4)
            ck("kv1")
            kv_pass(4, 3)
            ck("kv2")
            program_body()
        except _Stop:
            pass
        K.finalize()
    return nc


_NC_CACHE = {}


def _const_tables():
    f32 = np.float32
    g = np.array(GAMMA, dtype=np.float64)
    r = np.arange(128)
    dec = np.zeros((128, 6, 8), dtype=np.float64)
    dec[:, 0, :] = g[None, :] ** (r[:, None] + 1)
    dec[:, 1, :] = DK ** -0.5 * g[None, :] ** (-(r[:, None] + 1.0))
    i8 = r % 8
    dec[:, 2, :] = g[None, :] ** (i8[:, None] + 1)
    dec[:, 3, :] = DK ** -0.5 * g[None, :] ** (-(i8[:, None] + 1.0))
    dec[:, 4, :] = (g ** 128)[None, :]
    dec[:, 5, :] = (g ** 8)[None, :]
    ident = np.eye(128)
    j = r[:, None]
    i = r[None, :]
    cm_p = (i >= j).astype(np.float64)
    cm_s = ((i >= j) & (i // 8 == j // 8)).astype(np.float64)
    rowmask = (r[:, None] // 8 == np.arange(16)[None, :]).astype(np.float64)
    consts = np.concatenate([ident, cm_p, cm_s, rowmask, dec.reshape(128, 48)], axis=1).astype(f32)
    return consts


def _bands(pos0):
    W = (2, 4, 8, 16)
    tp = np.arange(128)[:, None]
    t = np.arange(128)[None, :]
    b = np.zeros((128, 4, 4, 128), dtype=np.float64)
    bh = np.zeros((120, 2, 4, 128), dtype=np.float64)
    eye = (tp == t).astype(np.float64)
    for wi, w in enumerate(W):
        win = ((t - tp >= 0) & (t - tp < w)).astype(np.float64)
        cnt = np.minimum(w, pos0 + np.arange(128) + 1).astype(np.float64)
        b[:, 0, wi, :] = win / cnt[None, :] - eye
        b[:, 1, wi, :] = win / w - eye
        b[:, 2, wi, :] = ((t + 128 - tp >= 0) & (t + 128 - tp < w)).astype(np.float64) / w
        b[:, 3, wi, :] = (win * (tp // 8 == t // 8)) / w - eye
        for half in range(2):
            for s8 in range(8):
                for rr in range(15):
                    for ii in range(8):
                        if ii + 15 - rr < w:
                            bh[s8 * 15 + rr, half, wi, (half * 8 + s8) * 8 + ii] = 1.0 / w
    return b.reshape(128, 2048).astype(np.float32), bh.reshape(120, 1024).astype(np.float32)


def _cs_table(pos):
    half = DK // 2
    theta = 10000.0 ** (-np.arange(half, dtype=np.float64) / half)
    ang = pos.astype(np.float64)[:, None] * theta[None, :]
    return np.concatenate([np.cos(ang), np.sin(ang)], axis=1).astype(np.float32)


def kernel(x_prompt, x_sample, state_pool, state_ret, state_conv,
           g_pre_mix, w_in, w_pool, pool_scale, gn_gain, w_out, g_post_mix,
           g_pre_ffn, w_up, conv_w, conv_b, w_down, g_post_ffn, _cores=None):
    f32 = np.float32
    A_ = lambda a: np.ascontiguousarray(np.asarray(a, dtype=f32))
    x_prompt, x_sample, state_pool, state_ret, state_conv = map(A_, (x_prompt, x_sample, state_pool, state_ret,
                                                                     state_conv))
    w_in, w_pool, w_out, w_up, w_down = map(A_, (w_in, w_pool, w_out, w_up, w_down))
    if "nc" not in _NC_CACHE:
        import os
        _NC_CACHE["nc"] = build_program(os.environ.get("KDEBUG"))
    nc = _NC_CACHE["nc"]

    consts = _const_tables()
    gvec = np.stack([A_(pool_scale), A_(gn_gain), A_(g_post_mix), A_(g_post_ffn)], axis=0)
    gT = np.concatenate([A_(g_pre_mix).reshape(16, 128).T, A_(g_pre_ffn).reshape(16, 128).T], axis=1)
    gT = np.ascontiguousarray(gT)
    cw = A_(conv_w).reshape(3, 88, 128)
    cb_ = A_(conv_b).reshape(1, 88, 128)
    convtab = np.ascontiguousarray(np.concatenate([cw, cb_], axis=0).transpose(2, 1, 0)).reshape(128, 352)
    cores = list(range(NCORES)) if _cores is None else list(_cores)
    in_maps = []
    band_cache = {}
    for c in cores:
        b, half = c // 2, c % 2
        if half == 1:
            xall = x_prompt[b]
            pos_all = np.arange(2048)
        else:
            xall = np.concatenate([np.zeros((1024, D), f32), x_prompt[b, :1024]], axis=0)
            pos_all = np.arange(2048) - 1024
        pos_all = np.maximum(pos_all, 0)
        cs = np.stack([_cs_table(pos_all[t * 128:(t + 1) * 128]) for t in range(16)]
                      + [_cs_table(16384 + (np.arange(128) % 8))], axis=0)
        if half not in band_cache:
            band_cache[half] = _bands(half * 1024)
        bands, bandh = band_cache[half]
        spc = state_pool[16 * c:16 * c + 16]
        in_maps.append({
            "xall": np.ascontiguousarray(xall),
            "xs": np.ascontiguousarray(x_sample[16 * c:16 * c + 16].reshape(128, D)),
            "sp": np.ascontiguousarray(spc.reshape(2, 120, 1024).transpose(1, 0, 2)),
            "spraw": np.ascontiguousarray(spc),
            "sr": np.ascontiguousarray(state_ret[16 * c:16 * c + 16]),
            "sc": np.ascontiguousarray(state_conv[16 * c:16 * c + 16].reshape(32, NIN)),
            "w_in": w_in, "w_pool": w_pool, "w_out": w_out, "w_up": w_up, "w_down": w_down,
            "gvec": gvec, "gT": gT, "convtab": convtab, "cs": cs, "consts": consts,
            "bands": bands, "bandh": bandh,
        })
    res = run_bass_kernel_spmd(nc, in_maps, core_ids=list(range(len(cores))))
    if _cores is not None:
        return res
    R = res.results
    y_prompt = np.zeros((4, 2048, D), f32)
    y_sample = np.zeros((128, 8, D), f32)
    npp = np.zeros((4, 15, 1024), f32)
    nrp = np.zeros((4, H, DK, DV), f32)
    ncp = np.zeros((4, 2, NIN), f32)
    nps = np.zeros((128, 15, 1024), f32)
    nrs = np.zeros((128, H, DK, DV), f32)
    ncs = np.zeros((128, 2, NIN), f32)
    for c in range(NCORES):
        b, half = c // 2, c % 2
        r = R[c]
        y_prompt[b, half * 1024:(half + 1) * 1024] = r["y_main"]
        y_sample[16 * c:16 * c + 16] = r["y_s"].reshape(16, 8, D)
        nps[16 * c:16 * c + 16] = r["np_s"]
        nrs[16 * c:16 * c + 16] = r["nr_s"]
        ncs[16 * c:16 * c + 16] = r["nc_s"].reshape(16, 2, NIN)
        if half == 1:
            npp[b] = r["np_p"]
            nrp[b] = r["nr_p"]
            ncp[b] = r["nc_p"]
    return (y_prompt, y_sample, npp, nrp, ncp, nps, nrs, ncs)
```

```python
from contextlib import ExitStack

import numpy as np
import concourse.bass as bass
import concourse.mybir as mybir
from concourse.bass_utils import run_bass_kernel_spmd

F32 = mybir.dt.float32
BF16 = mybir.dt.bfloat16
AF = mybir.ActivationFunctionType
ALU = mybir.AluOpType

D = 2048
NIN = 11264
DFF = 5632
H = 8
DK = 128
DV = 256
EPS = 1e-6
NJ = 44
NCORES = 8
GAMMA = [1.0 - 2.0 ** (-5.0 - h) for h in range(H)]
GC_P = [g ** 128 for g in GAMMA]
GC_S = [g ** 8 for g in GAMMA]

CB_ORDER_FULL = [2, 3, 4, 5, 6, 7, 8, 9, 0, 1] + list(range(10, 22))
CB_ORDER_KV = [4, 5, 6, 7, 8, 9]


class Trk:
    __slots__ = ("name", "w", "r", "excl")

    def __init__(self, name, excl=False):
        self.name = name
        self.w = None
        self.r = {}
        self.excl = excl


class Slot:
    __slots__ = ("name", "trks", "sem", "count")

    def __init__(self, name, trks, sem):
        self.name = name
        self.trks = trks
        self.sem = sem
        self.count = 0


class Eng:
    def __init__(self, key, sem):
        self.key = key
        self.sem = sem
        self.count = 0
        self.ops = []
        self.seen = {}


class Builder:
    def __init__(self, nc, es):
        self.nc = nc
        self.es = es
        self.eng = {}
        for k in ("pe", "act", "dve", "pool", "sp"):
            self.eng[k] = Eng(k, es.enter_context(nc.semaphore("sem_" + k)))
        self.slots = []
        self.out_slots = []

    def trk(self, name, excl=False):
        return Trk(name, excl)

    def slot(self, name, trks):
        s = Slot(name, list(trks), self.es.enter_context(self.nc.semaphore("ds_" + name)))
        self.slots.append(s)
        return s

    def _deps(self, e, reads, writes):
        best = {}

        def add(ev):
            if ev is None:
                return
            sem, val, key = ev
            if key == "pe" and e.key == "pe":
                return
            k = id(sem)
            if k not in best or best[k][1] < val:
                best[k] = ev

        for t in reads:
            add(t.w)
        for t in writes:
            add(t.w)
            for ev in t.r.values():
                add(ev)
        waits = []
        for k, (sem, val, key) in best.items():
            if e.seen.get(k, 0) >= val:
                continue
            e.seen[k] = val
            waits.append((sem, val))
        return waits

    @staticmethod
    def _record(ev, reads, writes):
        k = id(ev[0])
        for t in reads:
            t.r[k] = ev
        for t in writes:
            t.w = ev
            t.r = {}

    def op(self, ek, fn, reads=(), writes=()):
        e = self.eng[ek]
        if any(t.excl for t in reads):
            writes = list(writes) + [t for t in reads if t.excl and t not in writes]
            reads = [t for t in reads if not t.excl]
        waits = self._deps(e, reads, writes)
        e.count += 1
        ev = (e.sem, e.count, ek)
        self._record(ev, reads, writes)
        e.ops.append((waits, fn, e.sem, 1))

    def dma(self, qk, out, in_, slot, kind, extra_reads=(), is_output=False, extra_writes=()):
        q = self.eng[qk]
        if kind == "load":
            reads, writes = list(extra_reads), list(slot.trks) + list(extra_writes)
        else:
            reads, writes = list(slot.trks) + list(extra_reads), list(extra_writes)
        waits = self._deps(q, reads, writes)
        slot.count += 16
        ev = (slot.sem, slot.count, None)
        self._record(ev, reads, writes)
        q.ops.append((waits, (lambda e, o=out, i=in_: e.dma_start(out=o, in_=i)), slot.sem, 16))
        if is_output and slot not in self.out_slots:
            self.out_slots.append(slot)

    def wait_slot_all(self, slot):
        for e in self.eng.values():
            if e.key in ("sp", "pool"):
                continue
            k = id(slot.sem)
            if e.seen.get(k, 0) >= slot.count:
                continue
            e.seen[k] = slot.count
            e.ops.append(([(slot.sem, slot.count)], None, None, 0))

    def finalize(self):
        nc = self.nc
        sp = self.eng["sp"]
        fin = [(s.sem, s.count) for s in self.out_slots]
        handles = {"pe": "tensor", "act": "scalar", "dve": "vector", "pool": "gpsimd", "sp": "sync"}
        block = self.es.enter_context(nc.Block())

        def make(ek):
            eng = self.eng[ek]

            def body(e):
                for waits, fn, sem, n in eng.ops:
                    for (ws, wv) in waits:
                        e.wait_ge(ws, wv)
                    if fn is None:
                        continue
                    ins = fn(e)
                    ins.then_inc(sem, n)
                if ek == "sp":
                    for (ws, wv) in fin:
                        e.wait_ge(ws, wv)
            return body

        for ek, hn in handles.items():
            getattr(block, hn)(make(ek))


def bc(ap, pos, n):
    dims = [list(d) for d in ap.ap]
    dims.insert(1 + pos, [0, n])
    return bass.AP(ap.tensor, ap.offset, dims)


class _Stop(Exception):
    pass


def build_program(debug=None):
    nc = bass.Bass("TRN2", target_bir_lowering=False)

    def ck(name):
        if debug is not None and debug in (name, name + "_x"):
            raise _Stop()

    def din(name, shape):
        return nc.dram_tensor(name, list(shape), F32, kind="ExternalInput").ap()

    def dout(name, shape):
        return nc.dram_tensor(name, list(shape), F32, kind="ExternalOutput").ap()

    xall = din("xall", [2048, D])
    xs = din("xs", [128, D])
    sp_in = din("sp", [120, 2, 1024])
    spraw = din("spraw", [16, 15, 1024])
    sr = din("sr", [16, H, DK, DV])
    sc = din("sc", [32, NIN])
    w_in = din("w_in", [D, NIN])
    w_pool = din("w_pool", [4, 256, 512])
    w_out = din("w_out", [D, D])
    w_up = din("w_up", [D, NIN])
    w_down = din("w_down", [DFF, D])
    gvec = din("gvec", [4, D])
    gT_in = din("gT", [128, 32])
    convtab_in = din("convtab", [128, 88 * 4])
    cs_in = din("cs", [17, 128, 128])
    consts_in = din("consts", [128, 448])
    bands_in = din("bands", [128, 4 * 512])
    bandh_in = din("bandh", [120, 2 * 512])

    wsc = nc.dram_tensor("wsc", [64, 128, 8192], BF16).ap()

    y_main = dout("y_main", [1024, D])
    y_s = dout("y_s", [128, D])
    np_p = dout("np_p", [15, 1024])
    nr_p = dout("nr_p", [H, DK, DV])
    nc_p = dout("nc_p", [2, NIN])
    np_s = dout("np_s", [16, 15, 1024])
    nr_s = dout("nr_s", [16, H, DK, DV])
    nc_s = dout("nc_s", [32, NIN])

    es = ExitStack()
    with es:
        off = [0]

        def alloc(nbytes):
            o = off[0]
            off[0] += (nbytes + 31) // 32 * 32
            return o

        A = {}
        for t in range(4):
            A["X%d" % t] = alloc(8192)
            A["Y%d" % t] = alloc(8192)
        A["hT"] = alloc(16384)
        A["WB0"] = alloc(16384)
        A["WB1"] = alloc(16384)
        A["WB2"] = alloc(16384)
        A["R"] = alloc(8192)
        A["carryU"] = alloc(2048)
        A["identb"] = alloc(256)
        A["consts"] = alloc(448 * 4)
        A["gT"] = alloc(128)
        A["convtab"] = alloc(88 * 4 * 4)
        A["hist"] = alloc(88 * 2 * 4)
        A["cs"] = alloc(4 * 512)
        A["small"] = alloc(64 * 4)
        A["gnst"] = alloc(8 * 6 * 4 + 8 * 2 * 4 + 64 + 64)
        A["xb0"] = alloc(4096)
        A["xb1"] = alloc(4096)
        shared_base = off[0]
        A["GB"] = alloc(8192)
        A["rot"] = alloc(2048)
        A["t1"] = alloc(1024)
        A["t2"] = alloc(1024)
        A["sg0"] = alloc(2048)
        A["sg1"] = alloc(2048)
        A["zT"] = alloc(2048)
        A["qkT"] = alloc(4096)
        A["STm"] = alloc(2048)
        A["Rbf"] = alloc(4096)
        A["wpool"] = alloc(8192)
        A["bands"] = alloc(4096)
        A["bandh"] = alloc(2048)
        A["Rb0"] = alloc(2048)
        A["Rb1"] = alloc(2048)
        A["qTm0"] = alloc(2048)
        A["qTm1"] = alloc(2048)
        A["kppm0"] = alloc(2048)
        A["kpp"] = A["kppm0"]
        A["kppm1"] = alloc(2048)
        A["spbf"] = A["qkT"]
        endA = off[0]
        off[0] = shared_base
        A["actT"] = alloc(22 * 512 * 2)
        for nm in ("upv0", "upv1", "upg0", "upg1"):
            A[nm] = alloc(516 * 4)
        for nm in ("usv0", "usv1", "usg0", "usg1"):
            A[nm] = alloc(640)
        for nm in ("cv0", "cv1", "cg0", "cg1"):
            A[nm] = alloc(2048)
        A["gl0"], A["gl1"] = A["cg0"], A["cg1"]
        A["scT"] = alloc(88 * 32 * 4)
        A["stg0"], A["stg1"] = A["cv0"], A["cv1"]
        endB = off[0]
        total = max(endA, endB)
        assert total <= 212800, (total, endA, endB)
        arena = es.enter_context(nc.sbuf_tensor("arena", [128, total // 4], F32))
        ps = es.enter_context(nc.psum_tensor("ps", [128, 4096], F32))

        def V(name, nbytes, dt=F32, pat=None, boff=0, **kw):
            o = A[name] + boff
            ap = arena[:, o // 4:(o + nbytes) // 4]
            if dt == BF16:
                ap = ap.bitcast(BF16)
            if pat:
                ap = ap.rearrange(pat, **kw)
            return ap

        def PSf(b0, nb=1):
            return ps[:, b0 * 512:(b0 + nb) * 512]

        def PSb(b0, nb=1):
            return ps[:, b0 * 512:(b0 + nb) * 512].bitcast(BF16)

        K = Builder(nc, es)

        PST = [K.trk("ps%d" % b, excl=True) for b in range(8)]
        Xq = [K.trk("Xq%d" % t) for t in range(4)]
        Xk = [K.trk("Xk%d" % t) for t in range(4)]
        Xv = [K.trk("Xv%d" % t) for t in range(4)]
        XT = [[Xq[t], Xk[t], Xv[t]] for t in range(4)]
        YT = [K.trk("Y%d" % t) for t in range(4)]
        hTT = [K.trk("hT%d" % t) for t in range(4)]
        NHB = 6
        HBT = [K.trk("HB%d" % k) for k in range(NHB)]
        T = {n: K.trk(n) for n in (
            "GB", "R", "carryU", "const", "hist", "cs", "small", "gnst", "xb0", "xb1", "rot", "t1", "t2",
            "sg0", "sg1", "zT", "qkT", "STm", "kpp", "Rbf", "wpool", "bands", "Rf0", "Rf1", "Rb0", "Rb1",
            "qTm0", "qTm1", "kppm0", "kppm1", "spbf", "actT", "upv0", "upv1", "upg0", "upg1", "usv0", "usv1",
            "usg0", "usg1", "cv0", "cv1", "cg0", "cg1", "gl0", "gl1", "scT", "stg0", "stg1")}
        T["kpp"] = T["kppm0"]
        T["spbf"] = T["qkT"]
        T["gl0"], T["gl1"] = T["cg0"], T["cg1"]
        T["stg0"], T["stg1"] = T["cv0"], T["cv1"]
        PHA = ["GB", "rot", "t1", "t2", "sg0", "sg1", "zT", "qkT", "STm", "kpp", "Rbf", "wpool", "bands", "Rf0", "Rf1",
               "Rb0", "Rb1", "qTm0", "qTm1", "kppm0", "kppm1", "spbf"]
        PHB = ["actT", "upv0", "upv1", "upg0", "upg1", "usv0", "usv1", "usg0", "usg1", "cv0", "cv1", "cg0", "cg1",
               "gl0", "gl1", "scT", "stg0", "stg1"]

        actTT = [K.trk("actT%d" % j) for j in range(22)]
        for j in range(4):
            T["small%d" % j] = K.trk("small%d" % j)
        for j in range(22):
            T["actT%d" % j] = actTT[j]
        PHB += ["actT%d" % j for j in range(22)]
        S_X = [K.slot("X%d" % t, XT[t]) for t in range(4)]
        S_Y = [K.slot("Y%d" % t, [YT[t]]) for t in range(4)]
        S_HB = [K.slot("HB%d" % k, [HBT[k]]) for k in range(NHB)]
        S_HBst = [K.slot("HBst%d" % k, [HBT[k]]) for k in range(NHB)]
        S_GB = K.slot("GB", [T["GB"]])
        S_R = K.slot("R", [T["R"]])
        S_GBf = K.slot("GBf", [T["cv0"], T["cv1"], T["cg0"], T["cg1"]])
        S_const = K.slot("const", [T["const"]])
        S_cs = K.slot("cs", [T["cs"]])
        S_wpool = K.slot("wpool", [T["wpool"]])
        S_bands = K.slot("bands", [T["bands"]])
        S_spbf = K.slot("spbf", [T["spbf"]])
        S_Rb = [K.slot("Rb%d" % i, [T["Rb%d" % i]]) for i in range(2)]
        S_sg = [K.slot("sg%d" % i, [T["sg%d" % i]]) for i in range(2)]
        S_stg = [K.slot("stg%d" % i, [T["stg%d" % i]]) for i in range(2)]
        S_po = [K.slot("po%d" % i, [T["sg%d" % i]]) for i in range(2)]
        S_misc = K.slot("misc", [])
        YH = [K.trk("YH%d" % i) for i in range(8)]
        S_YH = [K.slot("YH%d" % i, [YH[i]]) for i in range(8)]
        S_YHst = [K.slot("YHst%d" % i, [YH[i]]) for i in range(8)]

        Xf = [V("X%d" % t, 8192) for t in range(4)]
        Xqb = [V("X%d" % t, 2048, BF16) for t in range(4)]
        Xkb = [V("X%d" % t, 2048, BF16, boff=2048) for t in range(4)]
        Xvb = [V("X%d" % t, 4096, BF16, boff=4096) for t in range(4)]
        Yf = [V("Y%d" % t, 8192) for t in range(4)]
        Yub = [V("Y%d" % t, 2048, BF16, boff=6144) for t in range(4)]
        YHv = [V("Y%d" % (i // 2), 4096, F32, "p (h v) -> p h v", h=4, boff=4096 * (i % 2)) for i in range(8)]
        hT = V("hT", 16384, BF16, "p (k n) -> p k n", k=16)
        HBv = [V("WB0", 8192, BF16, "p (k n) -> p k n", k=8, boff=8192 * k) for k in range(NHB)]
        HBf = [V("WB0", 8192, BF16, boff=8192 * k) for k in range(NHB)]

        def hb(i, h):
            return (2 * i + h) % NHB
        GB = V("GB", 8192)
        GBf = V("cv0", 8192)
        Rf32 = V("R", 8192, F32, "p (h v) -> p h v", h=8)
        carryU = V("carryU", 2048, BF16)
        identb = V("identb", 256, BF16)
        consts = V("consts", 448 * 4)
        identf = consts[:, 0:128]
        cmask = [consts[:, 128:256], consts[:, 256:384]]
        rowmask = consts[:, 384:400]
        dec = consts[:, 400:448].rearrange("p (a h) -> p a h", a=6)
        gT = V("gT", 128, F32, "p (a k) -> p a k", a=2)
        convtab = V("convtab", 88 * 16, F32, "p (j c) -> p j c", c=4)
        hist = V("hist", 88 * 8, F32, "p (j c) -> p j c", c=2)
        cst = V("cs", 2048, F32, "p (t c) -> p t c", t=4)
        small = V("small", 256)
        st6 = V("gnst", 192, F32, "p (h c) -> p h c", c=6)
        mv = V("gnst", 64, F32, "p (h c) -> p h c", c=2, boff=192)
        rs8 = V("gnst", 32, F32, boff=256)
        nm8 = V("gnst", 32, F32, boff=288)
        xb = [V("xb%d" % i, 4096, BF16) for i in range(2)]
        rot = V("rot", 2048, F32, "p (h d) -> p h d", h=4)
        t1 = V("t1", 1024, F32, "p (h d) -> p h d", h=4)
        t2 = V("t2", 1024, F32, "p (h d) -> p h d", h=4)
        sg = [V("sg%d" % i, 2048) for i in range(2)]
        zT = V("zT", 2048, BF16, "p (c n) -> p c n", c=8)
        qkT = V("qkT", 4096, BF16, "p (c n) -> p c n", c=16)
        STm = V("STm", 2048, BF16, "p (h n) -> p h n", h=8)
        kpp = V("kpp", 2048, BF16, "p (h n) -> p h n", h=8)
        Rbf = V("Rbf", 4096, BF16, "p (h v) -> p h v", h=8)
        wpool = V("wpool", 8192, BF16, "p (c n) -> p c n", c=8)
        bands = V("bands", 4096, BF16, "p (b w n) -> p b w n", b=4, w=4)
        bandh = V("bandh", 2048, BF16, "p (b w n) -> p b w n", b=2, w=4)
        RbS = [V("Rb%d" % i, 2048, BF16, "p (h v) -> p h v", h=4) for i in range(2)]
        qTm = [V("qTm%d" % i, 2048, BF16, "p (h n) -> p h n", h=8) for i in range(2)]
        kppm = [V("kppm%d" % i, 2048, BF16, "p (h n) -> p h n", h=8) for i in range(2)]
        spbf = V("spbf", 4096, BF16, "p (a n) -> p a n", a=2)
        actT = V("actT", 22 * 1024, BF16, "p (j n) -> p j n", j=22)
        upb = {("v", i): V("upv%d" % i, 516 * 4) for i in range(2)}
        upb.update({("g", i): V("upg%d" % i, 516 * 4) for i in range(2)})
        usb = {("v", i): V("usv%d" % i, 640, F32, "p (s c) -> p s c", s=16) for i in range(2)}
        usb.update({("g", i): V("usg%d" % i, 640, F32, "p (s c) -> p s c", s=16) for i in range(2)})
        cvb = {("v", i): V("cv%d" % i, 2048) for i in range(2)}
        cvb.update({("g", i): V("cg%d" % i, 2048) for i in range(2)})
        glb = [V("gl%d" % i, 2048) for i in range(2)]
        scT = V("scT", 88 * 128, F32, "p (j c) -> p j c", c=32)
        stg = [V("stg%d" % i, 2048) for i in range(2)]

        w_in_v = w_in.rearrange("(kc p) n -> p kc n", p=128)
        w_out_v = w_out.rearrange("(kc p) n -> p kc n", p=128)
        w_up_v = w_up.rearrange("(kc p) n -> p kc n", p=128)
        w_down_v = w_down.rearrange("(kc p) n -> p kc n", p=128)

        def wplan():
            items = []
            for _ in range(2):
                items += [("in", cb) for cb in CB_ORDER_KV]
            for _ in range(3):
                items += [("in", cb) for cb in CB_ORDER_FULL]
                items += [("out", cb) for cb in range(4)]
                for hf in range(2):
                    items += [("up", hf, g) for g in range(11)]
                    items += [("down", hf, nb, q) for nb in range(4) for q in range(2)]
            return items

        WITEMS = wplan()
        wstate = {"issued": 0, "used": 0}

        wsc_trk = {}

        def w_index(it):
            if it[0] == "in":
                return it[1], 8192
            if it[0] == "out":
                return 22 + it[1], 8192
            if it[0] == "up":
                return 26 + it[1] * 11 + it[2], 8192
            return 48 + it[1] * 8 + it[2] * 2 + it[3], 5632

        DKH = [(0, 8), (8, 11)]

        w_first = {}

        def w_issue_half(i, h):
            it = WITEMS[i]
            k = hb(i, h)
            idx, n = w_index(it)
            if (i, 0) not in w_first and (i, 1) not in w_first:
                if it in wsc_trk:
                    w_first[(i, 0)] = w_first[(i, 1)] = None
                else:
                    defer = (12 <= i < 76) and ((i - 12) % 3 == 2)
                    w_first[(i, 0)] = w_first[(i, 1)] = not defer
                    if not defer:
                        wsc_trk[it] = [K.trk("wsc%d_%d" % (idx, hh)) for hh in range(2)]
            first = w_first[(i, h)]
            if it[0] == "down":
                nk = DKH[h][1] - DKH[h][0]
                f0, ln = DKH[h][0] * 512, nk * 512
            else:
                nk = 8
                f0, ln = h * 4096, 4096
            if first is None:
                K.dma("pool", HBf[k][:, 0:ln], wsc[idx][:, f0:f0 + ln], S_HB[k], "load",
                      extra_reads=[wsc_trk[it][h]])
                return
            ks = slice(8 * h, 8 * h + 8)
            if it[0] == "in":
                K.dma("pool", HBv[k], w_in_v[:, ks, it[1] * 512:(it[1] + 1) * 512], S_HB[k], "load")
            elif it[0] == "out":
                K.dma("pool", HBv[k], w_out_v[:, ks, it[1] * 512:(it[1] + 1) * 512], S_HB[k], "load")
            elif it[0] == "up":
                j0 = it[1] * 22 + 2 * it[2]
                K.dma("pool", HBv[k][:, :, 0:256], w_up_v[:, ks, j0 * 128:j0 * 128 + 256], S_HB[k], "load")
                K.dma("pool", HBv[k][:, :, 256:512], w_up_v[:, ks, DFF + j0 * 128:DFF + j0 * 128 + 256],
                      S_HB[k], "load")
            else:
                _, hf, nb, q = it
                k0 = hf * 22 + q * 11
                K.dma("pool", HBv[k][:, 0:nk, :],
                      w_down_v[:, k0 + DKH[h][0]:k0 + DKH[h][1], nb * 512:(nb + 1) * 512], S_HB[k], "load")
            if first:
                K.dma("sp", wsc[idx][:, f0:f0 + ln], HBf[k][:, 0:ln], S_HBst[k], "store",
                      extra_writes=[wsc_trk[it][h]])

        whalf = {"next": 0}

        def w_issue_upto(code):
            while whalf["next"] <= min(code, 2 * len(WITEMS) - 1):
                c = whalf["next"]
                w_issue_half(c // 2, c % 2)
                whalf["next"] += 1

        def w_acquire(key):
            i = wstate["used"]
            assert WITEMS[i] == key, (WITEMS[i], key)
            w_issue_upto(2 * i + NHB - 1)
            wstate["used"] += 1
            wstate["cur"] = i
            return i

        def w_half_done(h):
            i = wstate["cur"]
            w_issue_upto(2 * i + h + NHB)

        K.dma("sp", consts, consts_in, S_const, "load")
        K.dma("sp", V("gT", 128), gT_in, S_const, "load")
        K.dma("sp", V("convtab", 88 * 16), convtab_in, S_const, "load")
        S_identb = K.slot("identb", [T["const"]])
        K.dma("pool", identb, consts_in[:, 0:128], S_identb, "load")
        K.wait_slot_all(S_const)
        K.wait_slot_all(S_identb)
        K.op("dve", lambda e: e.memset(V("hist", 88 * 8), 0.0), writes=[T["hist"]])
        K.op("dve", lambda e: e.memset(V("R", 8192), 0.0), writes=[T["R"]])
        K.op("dve", lambda e: e.memset(V("Rbf", 4096, BF16), 0.0), writes=[T["Rbf"]])

        sm_idx = [0]

        def sm_col():
            i = sm_idx[0] % 60
            sm_idx[0] += 1
            return small[:, i:i + 1]

        xb_i = [0]
        tr_i = [0]
        sg_i = [0]
        bank_i = [0]

        def next_bank():
            b = bank_i[0] % 8
            bank_i[0] += 1
            return b

        junkA = V("qkT", 4096, BF16)

        def stats(src_f32, src_trks, junk, junk_trk, ti):
            ss = sm_col()
            rs = sm_col()
            st = T["small%d" % (ti % 4)]
            K.op("act", lambda e: e.activation(out=junk, in_=src_f32, func=AF.Square, accum_out=ss),
                 reads=src_trks, writes=[junk_trk, st])
            K.op("act", lambda e: e.activation(out=rs, in_=ss, func=AF.Sqrt, scale=1.0 / D, bias=EPS),
                 reads=[st], writes=[st])
            K.op("dve", lambda e: e.reciprocal(out=rs, in_=rs), reads=[st], writes=[st])
            return rs, st

        def to_T(src_f32, src_trks, rs, st, tcol, gidx):
            i = xb_i[0] % 2
            xb_i[0] += 1
            xbt = T["xb%d" % i]
            K.op("act", lambda e: e.activation(out=xb[i], in_=src_f32, func=AF.Copy, scale=rs),
                 reads=list(src_trks) + [st], writes=[xbt])
            pb = 4 + 2 * (tr_i[0] % 2)
            tr_i[0] += 1
            pv = PSb(pb, 2)

            def tr(e):
                for kc in range(16):
                    ins = e.transpose(out=pv[:, kc * 128:(kc + 1) * 128], in_=xb[i][:, kc * 128:(kc + 1) * 128],
                                      identity=identb)
                return ins
            K.op("pe", tr, reads=[xbt], writes=[PST[pb], PST[pb + 1]])
            dst = hT[:, :, tcol * 128:(tcol + 1) * 128]
            src = pv.rearrange("p (k n) -> p k n", k=16)
            gb_ = bc(gT[:, gidx, :], 1, 128)
            K.op("dve", lambda e: e.tensor_tensor(out=dst, in0=src, in1=gb_, op=ALU.mult),
                 reads=[PST[pb], PST[pb + 1]], writes=[hTT[tcol]])

        def mm_half(wslot, h, tcol, bank):
            def f(e):
                for kc in range(8 * h, 8 * h + 8):
                    ins = e.matmul(PSf(bank), lhsT=hT[:, kc, tcol * 128:(tcol + 1) * 128],
                                   rhs=HBv[hb(wslot, h)][:, kc - 8 * h, :], start=(kc == 0), stop=(kc == 15))
                return ins
            K.op("pe", f, reads=[hTT[tcol], HBT[hb(wslot, h)]], writes=[PST[bank]])

        item_par = [0]

        def mm_item(wslot, ntl):
            base = 4 * (item_par[0] % 2)
            item_par[0] += 1
            banks = [base + i for i in range(ntl)]
            for h in range(2):
                for i in range(ntl):
                    mm_half(wslot, h, i, banks[i])
                w_half_done(h)
            return banks

        def rotary(bank, slot_i, heads0, dec_idx, dst, dst_trk, cs_slot):
            pv = PSf(bank).rearrange("p (h d) -> p h d", h=4)
            dq = bc(dec[:, dec_idx, heads0:heads0 + 4], 1, 128)
            K.op("dve", lambda e: e.tensor_tensor(out=rot, in0=pv, in1=dq, op=ALU.mult),
                 reads=[PST[bank]], writes=[T["rot"]])
            cosb = bc(cst[:, cs_slot, 0:64], 0, 4)
            sinb = bc(cst[:, cs_slot, 64:128], 0, 4)
            x1 = rot[:, :, 0:64]
            x2 = rot[:, :, 64:128]
            dv = dst.rearrange("p (h d) -> p h d", h=8)[:, heads0:heads0 + 4, :]
            K.op("dve", lambda e: e.tensor_tensor(out=t1, in0=x1, in1=cosb, op=ALU.mult), reads=[T["rot"], T["cs"]],
                 writes=[T["t1"]])
            K.op("dve", lambda e: e.tensor_tensor(out=t2, in0=x2, in1=sinb, op=ALU.mult), reads=[T["rot"], T["cs"]],
                 writes=[T["t2"]])
            K.op("dve", lambda e: e.tensor_tensor(out=dv[:, :, 0:64], in0=t1, in1=t2, op=ALU.subtract),
                 reads=[T["t1"], T["t2"]], writes=[dst_trk])
            K.op("dve", lambda e: e.tensor_tensor(out=t1, in0=x2, in1=cosb, op=ALU.mult), reads=[T["rot"]],
                 writes=[T["t1"]])
            K.op("dve", lambda e: e.tensor_tensor(out=t2, in0=x1, in1=sinb, op=ALU.mult), reads=[T["rot"]],
                 writes=[T["t2"]])
            K.op("dve", lambda e: e.tensor_tensor(out=dv[:, :, 64:128], in0=t1, in1=t2, op=ALU.add),
                 reads=[T["t1"], T["t2"]], writes=[dst_trk])

        def load_gain(idx):
            K.dma("sp", GB, gvec[idx].partition_broadcast(128), S_GB, "load")

        def kpp_prep(t):
            gcb = bc(dec[:, 4, :], 1, 128)
            kv = Xkb[t].rearrange("p (h d) -> p h d", h=8)
            K.op("dve", lambda e: e.tensor_tensor(out=kpp, in0=kv, in1=gcb, op=ALU.mult), reads=[Xk[t]],
                 writes=[T["kpp"]])

        def state_update_prompt(t, prep=True, mid=None):
            if prep:
                kpp_prep(t)

            def f(e):
                for h in range(8):
                    ins = e.matmul(ps[:, 2048 + h * 256:2048 + (h + 1) * 256], lhsT=kpp[:, h, :],
                                   rhs=Xvb[t][:, h * 256:(h + 1) * 256], start=True, stop=True)
                return ins
            K.op("pe", f, reads=[T["kpp"], Xv[t]], writes=PST[4:8])
            if mid is not None:
                mid()
            for h in range(8):
                K.op("dve", lambda e, h=h: e.scalar_tensor_tensor(
                    out=Rf32[:, h, :], in0=Rf32[:, h, :], scalar=GC_P[h], in1=ps[:, 2048 + h * 256:2048 + (h + 1) * 256],
                    op0=ALU.mult, op1=ALU.add), reads=[PST[4 + h // 2]], writes=[T["R"]])
            K.op("act", lambda e: e.copy(out=Rbf, in_=Rf32), reads=[T["R"]], writes=[T["Rbf"]])

        def qk_transposes(t):
            pv = PSb(4, 2).rearrange("p (c n) -> p c n", c=16)

            def f(e):
                for h in range(8):
                    e.transpose(out=pv[:, h, :], in_=Xqb[t][:, h * 128:(h + 1) * 128], identity=identb)
                for h in range(8):
                    ins = e.transpose(out=pv[:, 8 + h, :], in_=Xkb[t][:, h * 128:(h + 1) * 128], identity=identb)
                return ins
            K.op("pe", f, reads=[Xq[t], Xk[t]], writes=PST[4:6])
            K.op("dve", lambda e: e.tensor_copy(out=qkT, in_=pv), reads=PST[4:6], writes=[T["qkT"]])

        def scores(t, mask_idx):
            pv = PSf(6, 2).rearrange("p (h n) -> p h n", h=8)

            def f(e):
                for h in range(8):
                    ins = e.matmul(pv[:, h, :], lhsT=qkT[:, 8 + h, :], rhs=qkT[:, h, :], start=True, stop=True)
                return ins
            K.op("pe", f, reads=[T["qkT"]], writes=PST[6:8])
            mb_ = bc(cmask[mask_idx], 0, 8)
            K.op("dve", lambda e: e.tensor_tensor(out=STm, in0=pv, in1=mb_, op=ALU.mult), reads=PST[6:8],
                 writes=[T["STm"]])

        def o_evac(t):
            for b in range(4):
                K.op("act", lambda e, b=b: e.copy(out=Xf[t][:, b * 512:(b + 1) * 512], in_=PSf(b)),
                     reads=[PST[b]], writes=XT[t])

        def groupnorm_X(t):
            for h in range(8):
                K.op("dve", lambda e, h=h: e.bn_stats(out=st6[:, h, :], in_=Xf[t][:, h * 256:(h + 1) * 256]),
                     reads=XT[t], writes=[T["gnst"]])
            for h in range(8):
                K.op("dve", lambda e, h=h: e.bn_aggr(out=mv[:, h, :], in_=st6[:, h, :]), reads=[T["gnst"]],
                     writes=[T["gnst"]])
            K.op("act", lambda e: e.activation(out=rs8, in_=mv[:, :, 1], func=AF.Sqrt, scale=1.0, bias=EPS),
                 reads=[T["gnst"]], writes=[T["gnst"]])
            K.op("dve", lambda e: e.reciprocal(out=rs8, in_=rs8), reads=[T["gnst"]], writes=[T["gnst"]])
            K.op("dve", lambda e: e.scalar_tensor_tensor(out=nm8, in0=mv[:, :, 0], scalar=-1.0, in1=rs8,
                                                         op0=ALU.mult, op1=ALU.mult),
                 reads=[T["gnst"]], writes=[T["gnst"]])
            for h in range(8):
                K.op("act", lambda e, h=h: e.activation(out=Xf[t][:, h * 256:(h + 1) * 256],
                                                        in_=Xf[t][:, h * 256:(h + 1) * 256], func=AF.Identity,
                                                        scale=rs8[:, h:h + 1], bias=nm8[:, h:h + 1]),
                     reads=XT[t] + [T["gnst"]], writes=XT[t])
            K.op("dve", lambda e: e.tensor_tensor(out=Xf[t], in0=Xf[t], in1=GB, op=ALU.mult),
                 reads=XT[t] + [T["GB"]], writes=XT[t])

        def groupnorm_to_X(t):
            o_evac(t)
            groupnorm_X(t)

        def retention_prompt_a(t):
            qk_transposes(t)
            scores(t, 0)
            kpp_prep(t)

        def retention_prompt_b(t):
            def f(e):
                for h in range(8):
                    o = ps[:, h * 256:(h + 1) * 256]
                    e.matmul(o, lhsT=STm[:, h, :], rhs=Xvb[t][:, h * 256:(h + 1) * 256], start=True, stop=False)
                    ins = e.matmul(o, lhsT=qkT[:, h, :], rhs=Rbf[:, h, :], start=False, stop=True)
                return ins
            K.op("pe", f, reads=[T["STm"], Xv[t], T["qkT"], T["Rbf"]], writes=PST[0:4])
            state_update_prompt(t, prep=False, mid=lambda: o_evac(t))

        def retention_sample(t):
            qk_transposes(t)
            scores(t, 1)
            K.op("dve", lambda e: e.memset(ps[:, 0:2048], 0.0), writes=PST[0:4])
            for i in range(2):
                K.op("dve", lambda e, i=i: e.memset(qTm[i], 0.0), writes=[T["qTm%d" % i]])
            gcb = bc(dec[:, 5, :], 1, 128)
            kv = Xkb[t].rearrange("p (h d) -> p h d", h=8)
            sr_v = sr.rearrange("s h d v -> s d h v")
            nr_v = nr_s.rearrange("s h d v -> s d h v")
            def inherit(dst, src):
                n = 0
                evs = list(src.r.values()) + ([src.w] if src.w is not None else [])
                for ev in evs:
                    dst.r[("inh", id(src), n)] = ev
                    n += 1

            for i in range(8):
                inherit(YH[i], YT[i // 2])
            def prep(s):
                b2 = s % 2
                cols = slice(s * 8, (s + 1) * 8)
                K.op("dve", lambda e, b2=b2, cols=cols: e.tensor_copy(out=qTm[b2][:, :, cols], in_=qkT[:, 0:8, cols]),
                     reads=[T["qkT"]], writes=[T["qTm%d" % b2]])
                K.op("dve", lambda e, b2=b2, s=s: e.scalar_tensor_tensor(
                    out=kppm[b2], in0=kv, scalar=rowmask[:, s:s + 1], in1=gcb, op0=ALU.mult, op1=ALU.mult),
                    reads=[Xk[t]], writes=[T["kppm%d" % b2]])

            def rload(k):
                s, hh = k // 2, k % 2
                K.dma("sp", YHv[k % 8], sr_v[s, :, 4 * hh:4 * hh + 4, :], S_YH[k % 8], "load")

            for k in range(8):
                rload(k)
            prep(0)
            for s in range(16):
                b2 = s % 2
                cols = slice(s * 8, (s + 1) * 8)
                if s + 1 < 16:
                    prep(s + 1)
                for hh in range(2):
                    bi = (2 * s + hh) % 8
                    b = hh
                    rfv = YHv[bi]
                    K.op("act", lambda e, b=b, rfv=rfv: e.copy(out=RbS[b], in_=rfv), reads=[YH[bi]],
                         writes=[T["Rb%d" % b]])

                    def f(e, b=b, b2=b2, hh=hh):
                        for hl in range(4):
                            h = 4 * hh + hl
                            ins = e.matmul(ps[:, h * 256:(h + 1) * 256], lhsT=qTm[b2][:, h, :], rhs=RbS[b][:, hl, :],
                                           start=False, stop=False, skip_group_check=True)
                        return ins
                    K.op("pe", f, reads=[T["qTm%d" % b2], T["Rb%d" % b]], writes=PST[2 * hh:2 * hh + 2])
                    pb = 4 + 2 * hh

                    def f2(e, b2=b2, hh=hh, pb=pb):
                        for hl in range(4):
                            h = 4 * hh + hl
                            ins = e.matmul(ps[:, pb * 512 + hl * 256:pb * 512 + (hl + 1) * 256], lhsT=kppm[b2][:, h, :],
                                           rhs=Xvb[t][:, h * 256:(h + 1) * 256], start=True, stop=True)
                        return ins
                    K.op("pe", f2, reads=[T["kppm%d" % b2], Xv[t]], writes=PST[pb:pb + 2])
                for hh in range(2):
                    bi = (2 * s + hh) % 8
                    rfv = YHv[bi]
                    pb = 4 + 2 * hh
                    for hl in range(4):
                        h = 4 * hh + hl
                        K.op("dve", lambda e, rfv=rfv, hl=hl, h=h, pb=pb: e.scalar_tensor_tensor(
                            out=rfv[:, hl, :], in0=rfv[:, hl, :], scalar=GC_S[h],
                            in1=ps[:, pb * 512 + hl * 256:pb * 512 + (hl + 1) * 256], op0=ALU.mult, op1=ALU.add),
                            reads=[PST[pb + hl // 2]], writes=[YH[bi]])
                    K.dma("pool", nr_v[s, :, 4 * hh:4 * hh + 4, :], rfv, S_YHst[bi], "store", is_output=True)
                    if 2 * s + hh + 8 < 32:
                        rload(2 * s + hh + 8)
                K.op("dve", lambda e, b2=b2, cols=cols: e.memset(qTm[b2][:, :, cols], 0.0),
                     writes=[T["qTm%d" % b2]])
            for i in range(8):
                inherit(YT[i // 2], YH[i])

            def f3(e):
                for h in range(8):
                    ins = e.matmul(ps[:, h * 256:(h + 1) * 256], lhsT=STm[:, h, :], rhs=Xvb[t][:, h * 256:(h + 1) * 256],
                                   start=False, stop=True, skip_group_check=True)
                return ins
            K.op("pe", f3, reads=[T["STm"], Xv[t]], writes=PST[0:4])
            o_evac(t)

        def pooling(t, kind, prev_ap, prev_trk, band_cur_idx):
            pz = PSf(4, 2).rearrange("p (c n) -> p c n", c=8)

            def f(e):
                for cc in range(8):
                    g = cc // 2
                    if kind == "first":
                        ins = e.matmul(pz[:, cc, :], lhsT=Yub[t][:, cc * 128:(cc + 1) * 128],
                                       rhs=bands[:, band_cur_idx, g, :], start=True, stop=True)
                    elif kind == "prompt":
                        e.matmul(pz[:, cc, :], lhsT=Yub[t][:, cc * 128:(cc + 1) * 128],
                                 rhs=bands[:, band_cur_idx, g, :], start=True, stop=False)
                        ins = e.matmul(pz[:, cc, :], lhsT=prev_ap[:, cc * 128:(cc + 1) * 128],
                                       rhs=bands[:, 2, g, :], start=False, stop=True)
                    else:
                        e.matmul(pz[:, cc, :], lhsT=Yub[t][:, cc * 128:(cc + 1) * 128],
                                 rhs=bands[:, 3, g, :], start=True, stop=False)
                        e.matmul(pz[:, cc, :], lhsT=spbf[0:120, 0, cc * 128:(cc + 1) * 128],
                                 rhs=bandh[0:120, 0, g, :], start=False, stop=False)
                        ins = e.matmul(pz[:, cc, :], lhsT=spbf[0:120, 1, cc * 128:(cc + 1) * 128],
                                       rhs=bandh[0:120, 1, g, :], start=False, stop=True)
                return ins
            rd = [YT[t], T["bands"]]
            if kind == "prompt":
                rd.append(prev_trk)
            if kind == "sample":
                rd.append(T["spbf"])
            K.op("pe", f, reads=rd, writes=PST[4:6])
            K.op("act", lambda e: e.copy(out=zT, in_=pz), reads=PST[4:6], writes=[T["zT"]])

            def f2(e):
                for g in range(4):
                    for kc in range(2):
                        ins = e.matmul(PSf(g), lhsT=zT[:, 2 * g + kc, :], rhs=wpool[:, 2 * g + kc, :],
                                       start=(kc == 0), stop=(kc == 1))
                return ins
            K.op("pe", f2, reads=[T["zT"], T["wpool"]], writes=PST[0:4])
            K.op("act", lambda e: e.copy(out=Yf[t], in_=ps[:, 0:2048]), reads=PST[0:4], writes=[YT[t]])

        def load_phaseA_consts():
            K.dma("pool", V("bands", 4096, BF16), bands_in, S_bands, "load",
                  extra_reads=[])
            K.dma("pool", V("bandh", 2048, BF16)[0:120, :], bandh_in, S_bands, "load")
            load_gain(0)
            wv = w_pool.rearrange("g (kc p) n -> p g kc n", p=128)
            for g in range(4):
                i = sg_i[0] % 2
                sg_i[0] += 1
                for kc in range(2):
                    K.dma("sp", sg[i], wv[:, g, kc, :], S_sg[i], "load")
                    K.op("dve", lambda e, g=g, kc=kc, i=i: e.tensor_tensor(
                        out=wpool[:, 2 * g + kc, :], in0=sg[i], in1=GB[:, g * 512:(g + 1) * 512], op=ALU.mult),
                        reads=[T["sg%d" % i], T["GB"]], writes=[T["wpool"]])
            K.op("act", lambda e: e.copy(out=Rbf, in_=Rf32), reads=[T["R"]], writes=[T["Rbf"]])

        def alias_fence(new_names, old_names):
            evs_w = []
            for n in old_names:
                t = T[n]
                for nn in new_names:
                    tt = T[nn]
                    if t.w is not None:
                        tt.r[id(t.w[0]) + 1] = t.w
                    for k, ev in t.r.items():
                        tt.r[k + 2] = ev

        def kv_pass(tile0, ntiles):
            K.dma("sp", cst[:, 0:ntiles, :], cs_in[tile0:tile0 + ntiles].rearrange("t p c -> p t c"),
                  S_cs, "load")
            for t in range(ntiles):
                K.dma("sp", Xf[t], xall[(tile0 + t) * 128:(tile0 + t + 1) * 128, :], S_X[t], "load")
            rr = {0: stats(Xf[0], XT[0], junkA, T["qkT"], 0)}
            for t in range(ntiles):
                if t + 1 < ntiles:
                    rr[t + 1] = stats(Xf[t + 1], XT[t + 1], junkA, T["qkT"], t + 1)
                to_T(Xf[t], XT[t], rr[t][0], rr[t][1], t, 0)
            for cb in CB_ORDER_KV:
                wslot = w_acquire(("in", cb))
                banks = mm_item(wslot, ntiles)
                for t in range(ntiles):
                    bank = banks[t]
                    if cb in (4, 5):
                        rotary(bank, None, 4 * (cb - 4), 1, Xkb[t], Xk[t], t)
                    else:
                        K.op("act", lambda e, t=t, cb=cb, bank=bank: e.copy(
                            out=Xvb[t][:, (cb - 6) * 512:(cb - 5) * 512], in_=PSf(bank)),
                            reads=[PST[bank]], writes=[Xv[t]])
            for t in range(ntiles):
                state_update_prompt(t)

        pending_conv = []

        ssf = V("gnst", 64, F32, "p (t c) -> p t c", t=4, boff=192 + 128)

        def full_pass(pi, tiles):
            nt = len(tiles)
            N = nt * 128
            alias_fence(PHA, PHB)
            for i, tl in enumerate(tiles):
                K.dma("sp", Xf[i], tl["xsrc"], S_X[i], "load")
            for i, tl in enumerate(tiles):
                K.dma("sp", cst[:, i, :], cs_in[tl["cs"]], S_cs, "load")
            rr = {0: stats(Xf[0], XT[0], junkA, T["qkT"], 0)}
            for i in range(nt):
                if i + 1 < nt:
                    rr[i + 1] = stats(Xf[i + 1], XT[i + 1], junkA, T["qkT"], i + 1)
                to_T(Xf[i], XT[i], rr[i][0], rr[i][1], i, 0)
            if pending_conv:
                for a_ in pending_conv:
                    conv_state_out(*a_)
                del pending_conv[:]
                alias_fence(PHA, PHB)
            load_phaseA_consts()
            ck("p%d_ph1" % pi)
            for cb in CB_ORDER_FULL:
                wslot = w_acquire(("in", cb))
                banks = mm_item(wslot, nt)
                for i, tl in enumerate(tiles):
                    bank = banks[i]
                    smp = tl["kind"] == "sample"
                    if cb in (2, 3):
                        rotary(bank, None, 4 * (cb - 2), 2 if smp else 0, Xqb[i], Xq[i], i)
                    elif cb in (4, 5):
                        rotary(bank, None, 4 * (cb - 4), 3 if smp else 1, Xkb[i], Xk[i], i)
                    elif 6 <= cb <= 9:
                        K.op("act", lambda e, i=i, cb=cb, bank=bank: e.copy(
                            out=Xvb[i][:, (cb - 6) * 512:(cb - 5) * 512], in_=PSf(bank)),
                            reads=[PST[bank]], writes=[Xv[i]])
                    elif cb in (0, 1):
                        K.op("act", lambda e, i=i, cb=cb, bank=bank: e.copy(
                            out=Yub[i][:, cb * 512:(cb + 1) * 512], in_=PSf(bank)),
                            reads=[PST[bank]], writes=[YT[i]])
                        if tl.get("pool_out") is not None and debug != "nopoolout" and not (debug or "").endswith("_x"):
                            j = sg_i[0] % 2
                            sg_i[0] += 1
                            K.op("dve", lambda e, j=j, bank=bank: e.tensor_copy(out=sg[j], in_=PSf(bank)),
                                 reads=[PST[bank]], writes=[T["sg%d" % j]])
                            for (dst, p0, p1) in tl["pool_out"](cb):
                                K.dma("sp", dst, sg[j][p0:p1, :], S_po[j], "store", is_output=True)
                    elif 10 <= cb <= 13:
                        j = sg_i[0] % 2
                        sg_i[0] += 1
                        c0 = (cb - 10) * 512
                        K.op("act", lambda e, j=j, bank=bank: e.activation(out=sg[j], in_=PSf(bank), func=AF.Silu),
                             reads=[PST[bank]], writes=[T["sg%d" % j]])
                        K.op("dve", lambda e, i=i, j=j, c0=c0: e.tensor_tensor(
                            out=Xf[i][:, c0:c0 + 512], in0=Xf[i][:, c0:c0 + 512], in1=sg[j], op=ALU.mult),
                            reads=XT[i] + [T["sg%d" % j]], writes=XT[i])
                    elif 14 <= cb <= 17:
                        j = sg_i[0] % 2
                        sg_i[0] += 1
                        c0 = (cb - 14) * 512
                        K.op("act", lambda e, j=j, bank=bank: e.activation(out=sg[j], in_=PSf(bank),
                                                                           func=AF.Sigmoid),
                             reads=[PST[bank]], writes=[T["sg%d" % j]])
                        K.op("dve", lambda e, i=i, j=j, c0=c0: e.tensor_tensor(
                            out=Yf[i][:, c0:c0 + 512], in0=Yf[i][:, c0:c0 + 512], in1=sg[j], op=ALU.mult),
                            reads=[YT[i], T["sg%d" % j]], writes=[YT[i]])
                    else:
                        j = sg_i[0] % 2
                        sg_i[0] += 1
                        c0 = (cb - 18) * 512
                        K.op("act", lambda e, j=j, bank=bank: e.activation(out=sg[j], in_=PSf(bank),
                                                                           func=AF.Sigmoid),
                             reads=[PST[bank]], writes=[T["sg%d" % j]])
                        K.op("dve", lambda e, i=i, j=j, c0=c0: e.tensor_tensor(
                            out=sg[j], in0=Xf[i][:, c0:c0 + 512], in1=sg[j], op=ALU.mult),
                            reads=XT[i] + [T["sg%d" % j]], writes=[T["sg%d" % j]])
                        K.op("dve", lambda e, i=i, j=j, c0=c0: e.tensor_tensor(
                            out=Yf[i][:, c0:c0 + 512], in0=Yf[i][:, c0:c0 + 512], in1=sg[j], op=ALU.add),
                            reads=[YT[i], T["sg%d" % j]], writes=[YT[i]])
                if cb == 9:
                    load_gain(1)
                    pend = None
                    for i, tl in enumerate(tiles):
                        if tl["kind"] == "sample":
                            if pend is not None:
                                groupnorm_X(pend)
                            retention_sample(i)
                            pend = i
                        else:
                            retention_prompt_a(i)
                            if pend is not None:
                                groupnorm_X(pend)
                            retention_prompt_b(i)
                            pend = i
                    if pend is not None:
                        groupnorm_X(pend)
                    if tiles[-1].get("ret_out"):
                        K.dma("sp", nr_p.rearrange("h d v -> d h v"), Rf32, S_R, "store", is_output=True)
                    ck("p%d_ret" % pi)
                if cb == 1:
                    ck("p%d_cb1" % pi)
                    last = nt - 1
                    if any(tl["kind"] == "sample" for tl in tiles):
                        K.dma("pool", spbf[0:120, :, :], sp_in, S_spbf, "load")
                    tmpU = V("STm", 2048, BF16)
                    K.op("dve", lambda e: e.tensor_copy(out=tmpU, in_=Yub[last]), reads=[YT[last]],
                         writes=[T["STm"]])
                    for i in range(nt - 1, -1, -1):
                        tl = tiles[i]
                        if tl["kind"] == "sample":
                            pooling(i, "sample", None, None, 3)
                        elif tl["prev"] == "none":
                            pooling(i, "first", None, None, tl["band"])
                        elif tl["prev"] == "carry":
                            pooling(i, "prompt", carryU, T["carryU"], tl["band"])
                        else:
                            pooling(i, "prompt", Yub[tl["prev"]], YT[tl["prev"]], tl["band"])
                    K.op("dve", lambda e: e.tensor_copy(out=carryU, in_=tmpU), reads=[T["STm"]],
                         writes=[T["carryU"]])
                    ck("p%d_pool" % pi)
            ck("p%d_ph2" % pi)
            for i, tl in enumerate(tiles):
                j = xb_i[0] % 2
                xb_i[0] += 1
                K.op("act", lambda e, i=i, j=j: e.copy(out=xb[j], in_=Yf[i]), reads=[YT[i]], writes=[T["xb%d" % j]])
                pb = 4 + 2 * (tr_i[0] % 2)
                tr_i[0] += 1
                pv = PSb(pb, 2)

                def tr(e, j=j, pv=pv):
                    for kc in range(16):
                        ins = e.transpose(out=pv[:, kc * 128:(kc + 1) * 128], in_=xb[j][:, kc * 128:(kc + 1) * 128],
                                          identity=identb)
                    return ins
                K.op("pe", tr, reads=[T["xb%d" % j]], writes=[PST[pb], PST[pb + 1]])
                K.op("dve", lambda e, i=i, pv=pv: e.tensor_copy(out=hT[:, :, i * 128:(i + 1) * 128],
                                                                in_=pv.rearrange("p (k n) -> p k n", k=16)),
                     reads=[PST[pb], PST[pb + 1]], writes=[hTT[i]])
                K.dma("sp", Xf[i], tl["xsrc"], S_X[i], "load")
            ck("p%d_merge" % pi)
            load_gain(2)
            ssp = V("gnst", 64, F32, "p (t c) -> p t c", t=4, boff=192 + 128)
            for cb in range(4):
                wslot = w_acquire(("out", cb))
                banks = mm_item(wslot, nt)
                for i, tl in enumerate(tiles):
                    bank = banks[i]
                    K.op("act", lambda e, i=i, cb=cb, bank=bank: e.activation(
                        out=junkA[:, 0:512], in_=PSf(bank), func=AF.Square, accum_out=ssp[:, i, cb:cb + 1]),
                        reads=[PST[bank]], writes=[T["qkT"], T["small%d" % i]])
                    K.op("dve", lambda e, i=i, cb=cb, bank=bank: e.tensor_tensor(
                        out=Yf[i][:, cb * 512:(cb + 1) * 512], in0=PSf(bank), in1=GB[:, cb * 512:(cb + 1) * 512],
                        op=ALU.mult), reads=[PST[bank], T["GB"]], writes=[YT[i]])
            def tail4(i):
                st = T["small%d" % i]
                ss = sm_col()
                rs = sm_col()
                K.op("dve", lambda e, i=i, ss=ss: e.tensor_reduce(out=ss, in_=ssp[:, i, :], axis=mybir.AxisListType.X,
                                                                  op=ALU.add), reads=[st], writes=[st])
                K.op("act", lambda e, ss=ss, rs=rs: e.activation(out=rs, in_=ss, func=AF.Sqrt, scale=1.0 / D,
                                                                 bias=EPS), reads=[st], writes=[st])
                K.op("dve", lambda e, rs=rs: e.reciprocal(out=rs, in_=rs), reads=[st], writes=[st])
                K.op("dve", lambda e, i=i, rs=rs: e.scalar_tensor_tensor(out=Xf[i], in0=Yf[i], scalar=rs, in1=Xf[i],
                                                                         op0=ALU.mult, op1=ALU.add),
                     reads=XT[i] + [YT[i], st], writes=XT[i])

            tail4(0)
            rr = {0: stats(Xf[0], XT[0], junkA, T["qkT"], 0)}
            for i in range(nt):
                if i + 1 < nt:
                    tail4(i + 1)
                    rr[i + 1] = stats(Xf[i + 1], XT[i + 1], junkA, T["qkT"], i + 1)
                to_T(Xf[i], XT[i], rr[i][0], rr[i][1], i, 1)
            ck("p%d_ph4" % pi)
            alias_fence(PHB, PHA)
            has_sample = tiles[0]["kind"] == "sample"
            c0p = 128 if has_sample else 0
            Np = N - c0p
            if has_sample:
                for r in range(22):
                    j = r % 2
                    K.dma("sp", stg[j][0:32, :], sc[:, r * 512:(r + 1) * 512], S_stg[j], "load")
                    pb = 4 + 2 * (r % 2)

                    def f(e, j=j, pb=pb):
                        for c in range(4):
                            ins = e.transpose(out=ps[:, pb * 512 + c * 32:pb * 512 + (c + 1) * 32],
                                              in_=stg[j][0:32, c * 128:(c + 1) * 128], identity=identf[0:32, 0:32])
                        return ins
                    K.op("pe", f, reads=[T["stg%d" % j]], writes=[PST[pb]])
                    K.op("dve", lambda e, r=r, pb=pb: e.tensor_copy(
                        out=scT[:, r * 4:(r + 1) * 4, :],
                        in_=ps[:, pb * 512:pb * 512 + 128].rearrange("p (c n) -> p c n", c=4)),
                        reads=[PST[pb]], writes=[T["scT"]])
            ck("p%d_scT" % pi)
            for hf in range(2):
                up_i = 0
                for g in range(11):
                    wslot = w_acquire(("up", hf, g))
                    base = 4 * (item_par[0] % 2)
                    item_par[0] += 1
                    for h in range(2):
                        for jj in range(2):
                            for (kind, pb, woff) in (("v", base + 2 * jj, 0), ("g", base + 2 * jj + 1, 256)):
                                def f(e, pb=pb, woff=woff, jj=jj, wslot=wslot, h=h):
                                    for kc in range(8 * h, 8 * h + 8):
                                        ins = e.matmul(ps[:, pb * 512:pb * 512 + N],
                                                       lhsT=HBv[hb(wslot, h)][:, kc - 8 * h,
                                                                              woff + jj * 128:woff + (jj + 1) * 128],
                                                       rhs=hT[:, kc, 0:N], start=(kc == 0), stop=(kc == 15))
                                    return ins
                                K.op("pe", f, reads=[HBT[hb(wslot, h)]] + hTT[0:nt], writes=[PST[pb]])
                        w_half_done(h)
                    for jj in range(2):
                        jl = 2 * g + jj
                        j = hf * 22 + jl
                        pbv = base + 2 * jj
                        pbg = pbv + 1
                        rb = jj
                        for (kind, pb, woff, jidx) in (("v", pbv, 0, j), ("g", pbg, 256, NJ + j)):
                            ub = upb[(kind, rb)]
                            ut = T["up%s%d" % (kind, rb)]
                            cvv = cvb[(kind, rb)]
                            cvt = T["c%s%d" % (kind, rb)]
                            w0 = convtab[:, jidx, 0:1]
                            w1 = convtab[:, jidx, 1:2]
                            w2 = convtab[:, jidx, 2:3]
                            bb = convtab[:, jidx, 3:4]
                            K.op("act", lambda e, ub=ub, pb=pb: e.copy(out=ub[:, 2:2 + Np],
                                                                       in_=ps[:, pb * 512 + c0p:pb * 512 + N]),
                                 reads=[PST[pb]], writes=[ut])
                            K.op("dve", lambda e, ub=ub, jidx=jidx: e.tensor_copy(out=ub[:, 0:2], in_=hist[:, jidx, :]),
                                 reads=[T["hist"]], writes=[ut])
                            K.op("dve", lambda e, ub=ub, jidx=jidx: e.tensor_copy(out=hist[:, jidx, :],
                                                                                  in_=ub[:, Np:Np + 2]),
                                 reads=[ut], writes=[T["hist"]])
                            K.op("dve", lambda e, ub=ub, cvv=cvv, w2=w2, bb=bb: e.tensor_scalar(
                                out=cvv[:, c0p:N], in0=ub[:, 2:2 + Np], scalar1=w2, scalar2=bb, op0=ALU.mult,
                                op1=ALU.add), reads=[ut], writes=[cvt])
                            K.op("dve", lambda e, ub=ub, cvv=cvv, w1=w1: e.scalar_tensor_tensor(
                                out=cvv[:, c0p:N], in0=ub[:, 1:1 + Np], scalar=w1, in1=cvv[:, c0p:N], op0=ALU.mult,
                                op1=ALU.add), reads=[ut, cvt], writes=[cvt])
                            K.op("dve", lambda e, ub=ub, cvv=cvv, w0=w0: e.scalar_tensor_tensor(
                                out=cvv[:, c0p:N], in0=ub[:, 0:Np], scalar=w0, in1=cvv[:, c0p:N], op0=ALU.mult,
                                op1=ALU.add), reads=[ut, cvt], writes=[cvt])
                            if has_sample:
                                us = usb[(kind, rb)]
                                ust = T["us%s%d" % (kind, rb)]
                                K.op("act", lambda e, us=us, pb=pb: e.copy(
                                    out=us[:, :, 2:10],
                                    in_=ps[:, pb * 512:pb * 512 + 128].rearrange("p (s c) -> p s c", s=16)),
                                    reads=[PST[pb]], writes=[ust])
                                K.op("dve", lambda e, us=us, jidx=jidx: e.tensor_copy(
                                    out=us[:, :, 0:2], in_=scT[:, jidx, :].rearrange("p (s c) -> p s c", s=16)),
                                    reads=[T["scT"]], writes=[ust])
                                K.op("dve", lambda e, us=us, jidx=jidx: e.tensor_copy(
                                    out=scT[:, jidx, :].rearrange("p (s c) -> p s c", s=16), in_=us[:, :, 8:10]),
                                    reads=[ust], writes=[T["scT"]])
                                cs_ = cvv[:, 0:128].rearrange("p (s c) -> p s c", s=16)
                                K.op("dve", lambda e, us=us, cs_=cs_, w2=w2, bb=bb: e.tensor_scalar(
                                    out=cs_, in0=us[:, :, 2:10], scalar1=w2, scalar2=bb, op0=ALU.mult, op1=ALU.add),
                                    reads=[ust], writes=[cvt])
                                K.op("dve", lambda e, us=us, cs_=cs_, w1=w1: e.scalar_tensor_tensor(
                                    out=cs_, in0=us[:, :, 1:9], scalar=w1, in1=cs_, op0=ALU.mult, op1=ALU.add),
                                    reads=[ust, cvt], writes=[cvt])
                                K.op("dve", lambda e, us=us, cs_=cs_, w0=w0: e.scalar_tensor_tensor(
                                    out=cs_, in0=us[:, :, 0:8], scalar=w0, in1=cs_, op0=ALU.mult, op1=ALU.add),
                                    reads=[ust, cvt], writes=[cvt])
                        K.op("act", lambda e, rb=rb: e.activation(out=glb[rb][:, 0:N], in_=cvb[("g", rb)][:, 0:N],
                                                                  func=AF.Gelu_apprx_tanh),
                             reads=[T["cg%d" % rb]], writes=[T["gl%d" % rb]])
                        K.op("dve", lambda e, rb=rb, jl=jl: e.tensor_tensor(out=actT[:, jl, 0:N], in0=glb[rb][:, 0:N],
                                                                            in1=cvb[("v", rb)][:, 0:N], op=ALU.mult),
                             reads=[T["gl%d" % rb], T["cv%d" % rb]], writes=[actTT[jl]])
                ck("p%d_up%d" % (pi, hf))
                if hf == 1:
                    K.dma("sp", GBf, gvec[3].partition_broadcast(128), S_GBf, "load")
                out_tiles = [i for i, tl in enumerate(tiles) if tl.get("yout") is not None]
                for nb in range(4):
                    base = 0 if nb % 2 == 0 else 4
                    for q in range(2):
                        wslot = w_acquire(("down", hf, nb, q))
                        for h in range(2):
                            for i in out_tiles:
                                def f(e, i=i, q=q, wslot=wslot, base=base, h=h):
                                    for kk in range(DKH[h][0], DKH[h][1]):
                                        ins = e.matmul(PSf(base + i), lhsT=actT[:, q * 11 + kk, i * 128:(i + 1) * 128],
                                                       rhs=HBv[hb(wslot, h)][:, kk - DKH[h][0], :],
                                                       start=(q == 0 and kk == 0), stop=(q == 1 and kk == 10))
                                    return ins
                                K.op("pe", f, reads=actTT[q * 11 + DKH[h][0]:q * 11 + DKH[h][1]] + [HBT[hb(wslot, h)]],
                                     writes=[PST[base + i]])
                            w_half_done(h)
                    for i in out_tiles:
                        if hf == 0:
                            K.op("act", lambda e, i=i, nb=nb, base=base: e.copy(
                                out=Yf[i][:, nb * 512:(nb + 1) * 512], in_=PSf(base + i)),
                                reads=[PST[base + i]], writes=[YT[i]])
                        else:
                            ysl = Yf[i][:, nb * 512:(nb + 1) * 512]
                            K.op("dve", lambda e, ysl=ysl, i=i, base=base: e.tensor_tensor(
                                out=ysl, in0=ysl, in1=PSf(base + i), op=ALU.add),
                                reads=[PST[base + i], YT[i]], writes=[YT[i]])
                            K.op("act", lambda e, ysl=ysl, i=i, nb=nb: e.activation(
                                out=xb[0][:, 0:512], in_=ysl, func=AF.Square, accum_out=ssf[:, i, nb:nb + 1]),
                                reads=[YT[i]], writes=[T["xb0"], T["small%d" % i]])
                            K.op("dve", lambda e, ysl=ysl, nb=nb: e.tensor_tensor(
                                out=ysl, in0=ysl, in1=GBf[:, nb * 512:(nb + 1) * 512], op=ALU.mult),
                                reads=[YT[i], T["cv0"], T["cv1"], T["cg0"], T["cg1"]], writes=[YT[i]])
            ck("p%d_ffn" % pi)
            outs_ = [i for i, tl in enumerate(tiles) if tl.get("yout") is not None]
            for i in outs_:
                st = T["small%d" % i]
                ss = sm_col()
                rs = sm_col()
                K.op("dve", lambda e, i=i, ss=ss: e.tensor_reduce(out=ss, in_=ssf[:, i, :], axis=mybir.AxisListType.X,
                                                                  op=ALU.add), reads=[st], writes=[st])
                K.op("act", lambda e, ss=ss, rs=rs: e.activation(out=rs, in_=ss, func=AF.Sqrt, scale=1.0 / D,
                                                                 bias=EPS), reads=[st], writes=[st])
                K.op("dve", lambda e, rs=rs: e.reciprocal(out=rs, in_=rs), reads=[st], writes=[st])
                K.op("dve", lambda e, i=i, rs=rs: e.scalar_tensor_tensor(out=Yf[i], in0=Yf[i], scalar=rs, in1=Xf[i],
                                                                         op0=ALU.mult, op1=ALU.add),
                     reads=XT[i] + [YT[i], st], writes=[YT[i]])
                K.dma("sp", tiles[i]["yout"], Yf[i], S_Y[i], "store", is_output=True)
            ck("p%d_y" % pi)
            if has_sample:
                pending_conv.append((scT, 32, nc_s))
            if tiles[-1].get("conv_out"):
                conv_state_out(hist, 2, nc_p)

        def conv_state_out(src, width, dst):
            for r in range(22):
                j = r % 2
                pb = 4 + 2 * (r % 2)

                def f(e, r=r, pb=pb):
                    for c in range(4):
                        ins = e.transpose(out=ps[0:width, pb * 512 + c * 128:pb * 512 + (c + 1) * 128],
                                          in_=src[:, r * 4 + c, :], identity=identf)
                    return ins
                K.op("pe", f, reads=[T["scT"], T["hist"]], writes=[PST[pb]])
                K.op("dve", lambda e, j=j, pb=pb: e.tensor_copy(out=stg[j][0:width, :],
                                                                in_=ps[0:width, pb * 512:pb * 512 + 512]),
                     reads=[PST[pb]], writes=[T["stg%d" % j]])
                K.dma("sp", dst[:, r * 512:(r + 1) * 512], stg[j][0:width, :], S_stg[j], "store", is_output=True)

        def program_body():

            def np_s_out(cb):
                return [(np_s[:, 7:15, cb * 512:(cb + 1) * 512], 0, 128)]

            def np_p_out(cb):
                return [(np_p[:, cb * 512:(cb + 1) * 512], 113, 128)]

            def mtile(m):
                return dict(kind="prompt", xsrc=xall[(8 + m) * 128:(9 + m) * 128, :], cs=8 + m, band=1,
                            yout=y_main[m * 128:(m + 1) * 128, :])

            tS = dict(kind="sample", xsrc=xs, cs=16, band=3, prev=None, yout=y_s, pool_out=np_s_out)
            tC = dict(kind="prompt", xsrc=xall[7 * 128:8 * 128, :], cs=7, band=1, prev="none", yout=None)
            m = [mtile(i) for i in range(8)]
            m[0]["band"] = 0
            m[0]["prev"] = 1
            m[1]["prev"] = 2
            m[2]["prev"] = "carry"
            m[3]["prev"] = 0
            m[4]["prev"] = 1
            m[5]["prev"] = "carry"
            m[6]["prev"] = 0
            m[7]["prev"] = 1
            m[7]["pool_out"] = np_p_out
            m[7]["ret_out"] = True
            m[7]["conv_out"] = True
            K.dma("sp", np_s[:, 0:7, :], spraw[:, 8:15, :], S_misc, "load", is_output=True)
            ck("d2d")

            full_pass(0, [tS, tC, m[0], m[1]])
            full_pass(1, [m[2], m[3], m[4]])
            full_pass(2, [m[5], m[6], m[7]])


        try:
            ck("setup")
            kv_pass(0, 4)
            ck("kv1")
            kv_pass(4, 3)
            ck("kv2")
            program_body()
        except _Stop:
            pass
        K.finalize()
    return nc


_NC_CACHE = {}


def _const_tables():
    f32 = np.float32
    g = np.array(GAMMA, dtype=np.float64)
    r = np.arange(128)
    dec = np.zeros((128, 6, 8), dtype=np.float64)
    dec[:, 0, :] = g[None, :] ** (r[:, None] + 1)
    dec[:, 1, :] = DK ** -0.5 * g[None, :] ** (-(r[:, None] + 1.0))
    i8 = r % 8
    dec[:, 2, :] = g[None, :] ** (i8[:, None] + 1)
    dec[:, 3, :] = DK ** -0.5 * g[None, :] ** (-(i8[:, None] + 1.0))
    dec[:, 4, :] = (g ** 128)[None, :]
    dec[:, 5, :] = (g ** 8)[None, :]
    ident = np.eye(128)
    j = r[:, None]
    i = r[None, :]
    cm_p = (i >= j).astype(np.float64)
    cm_s = ((i >= j) & (i // 8 == j // 8)).astype(np.float64)
    rowmask = (r[:, None] // 8 == np.arange(16)[None, :]).astype(np.float64)
    consts = np.concatenate([ident, cm_p, cm_s, rowmask, dec.reshape(128, 48)], axis=1).astype(f32)
    return consts


def _bands(pos0):
    W = (2, 4, 8, 16)
    tp = np.arange(128)[:, None]
    t = np.arange(128)[None, :]
    b = np.zeros((128, 4, 4, 128), dtype=np.float64)
    bh = np.zeros((120, 2, 4, 128), dtype=np.float64)
    eye = (tp == t).astype(np.float64)
    for wi, w in enumerate(W):
        win = ((t - tp >= 0) & (t - tp < w)).astype(np.float64)
        cnt = np.minimum(w, pos0 + np.arange(128) + 1).astype(np.float64)
        b[:, 0, wi, :] = win / cnt[None, :] - eye
        b[:, 1, wi, :] = win / w - eye
        b[:, 2, wi, :] = ((t + 128 - tp >= 0) & (t + 128 - tp < w)).astype(np.float64) / w
        b[:, 3, wi, :] = (win * (tp // 8 == t // 8)) / w - eye
        for half in range(2):
            for s8 in range(8):
                for rr in range(15):
                    for ii in range(8):
                        if ii + 15 - rr < w:
                            bh[s8 * 15 + rr, half, wi, (half * 8 + s8) * 8 + ii] = 1.0 / w
    return b.reshape(128, 2048).astype(np.float32), bh.reshape(120, 1024).astype(np.float32)


def _cs_table(pos):
    half = DK // 2
    theta = 10000.0 ** (-np.arange(half, dtype=np.float64) / half)
    ang = pos.astype(np.float64)[:, None] * theta[None, :]
    return np.concatenate([np.cos(ang), np.sin(ang)], axis=1).astype(np.float32)


def kernel(x_prompt, x_sample, state_pool, state_ret, state_conv,
           g_pre_mix, w_in, w_pool, pool_scale, gn_gain, w_out, g_post_mix,
           g_pre_ffn, w_up, conv_w, conv_b, w_down, g_post_ffn, _cores=None):
    f32 = np.float32
    A_ = lambda a: np.ascontiguousarray(np.asarray(a, dtype=f32))
    x_prompt, x_sample, state_pool, state_ret, state_conv = map(A_, (x_prompt, x_sample, state_pool, state_ret,
                                                                     state_conv))
    w_in, w_pool, w_out, w_up, w_down = map(A_, (w_in, w_pool, w_out, w_up, w_down))
    if "nc" not in _NC_CACHE:
        import os
        _NC_CACHE["nc"] = build_program(os.environ.get("KDEBUG"))
    nc = _NC_CACHE["nc"]

    consts = _const_tables()
    gvec = np.stack([A_(pool_scale), A_(gn_gain), A_(g_post_mix), A_(g_post_ffn)], axis=0)
    gT = np.concatenate([A_(g_pre_mix).reshape(16, 128).T, A_(g_pre_ffn).reshape(16, 128).T], axis=1)
    gT = np.ascontiguousarray(gT)
    cw = A_(conv_w).reshape(3, 88, 128)
    cb_ = A_(conv_b).reshape(1, 88, 128)
    convtab = np.ascontiguousarray(np.concatenate([cw, cb_], axis=0).transpose(2, 1, 0)).reshape(128, 352)
    cores = list(range(NCORES)) if _cores is None else list(_cores)
    in_maps = []
    band_cache = {}
    for c in cores:
        b, half = c // 2, c % 2
        if half == 1:
            xall = x_prompt[b]
            pos_all = np.arange(2048)
        else:
            xall = np.concatenate([np.zeros((1024, D), f32), x_prompt[b, :1024]], axis=0)
            pos_all = np.arange(2048) - 1024
        pos_all = np.maximum(pos_all, 0)
        cs = np.stack([_cs_table(pos_all[t * 128:(t + 1) * 128]) for t in range(16)]
                      + [_cs_table(16384 + (np.arange(128) % 8))], axis=0)
        if half not in band_cache:
            band_cache[half] = _bands(half * 1024)
        bands, bandh = band_cache[half]
        spc = state_pool[16 * c:16 * c + 16]
        in_maps.append({
            "xall": np.ascontiguousarray(xall),
            "xs": np.ascontiguousarray(x_sample[16 * c:16 * c + 16].reshape(128, D)),
            "sp": np.ascontiguousarray(spc.reshape(2, 120, 1024).transpose(1, 0, 2)),
            "spraw": np.ascontiguousarray(spc),
            "sr": np.ascontiguousarray(state_ret[16 * c:16 * c + 16]),
            "sc": np.ascontiguousarray(state_conv[16 * c:16 * c + 16].reshape(32, NIN)),
            "w_in": w_in, "w_pool": w_pool, "w_out": w_out, "w_up": w_up, "w_down": w_down,
            "gvec": gvec, "gT": gT, "convtab": convtab, "cs": cs, "consts": consts,
            "bands": bands, "bandh": bandh,
        })
    res = run_bass_kernel_spmd(nc, in_maps, core_ids=list(range(len(cores))))
    if _cores is not None:
        return res
    R = res.results
    y_prompt = np.zeros((4, 2048, D), f32)
    y_sample = np.zeros((128, 8, D), f32)
    npp = np.zeros((4, 15, 1024), f32)
    nrp = np.zeros((4, H, DK, DV), f32)
    ncp = np.zeros((4, 2, NIN), f32)
    nps = np.zeros((128, 15, 1024), f32)
    nrs = np.zeros((128, H, DK, DV), f32)
    ncs = np.zeros((128, 2, NIN), f32)
    for c in range(NCORES):
        b, half = c // 2, c % 2
        r = R[c]
        y_prompt[b, half * 1024:(half + 1) * 1024] = r["y_main"]
        y_sample[16 * c:16 * c + 16] = r["y_s"].reshape(16, 8, D)
        nps[16 * c:16 * c + 16] = r["np_s"]
        nrs[16 * c:16 * c + 16] = r["nr_s"]
        ncs[16 * c:16 * c + 16] = r["nc_s"].reshape(16, 2, NIN)
        if half == 1:
            npp[b] = r["np_p"]
            nrp[b] = r["nr_p"]
            ncp[b] = r["nc_p"]
    return (y_prompt, y_sample, npp, nrp, ncp, nps, nrs, ncs)
```
